# Optimizing a Trainium2 kernel written in Bass

```python
import jax, jax.numpy as jnp
from jax import lax
import numpy as np

D_MODEL = 1024
BATCH = 8
SEQ = 2048
DEPTH = 1
DEC_BATCH = 128
DEC_SEQ = 1
PAST_LEN = 16384
PAGE_SIZE = 128

PLE_DIM = 256
D_FF = 2816
CHUNK = 128
A_GROUPS = 4
A_GROUP_DIM = 128
A_WIDTH = A_GROUPS * A_GROUP_DIM
B_HEADS = 4
B_DK = 128
B_DV = 256
B_KW = B_HEADS * B_DK
B_VW = B_HEADS * B_DV
GATE_RANK = 16
GATE_NORM = 16.0
GLA_BLOCK = 32
EPS = 1e-6
IN_SIZES = (A_WIDTH, A_WIDTH, B_KW, B_KW, B_VW, B_VW, GATE_RANK, D_MODEL, D_MODEL)
IN_COLS = sum(IN_SIZES)
IN_SPLITS = tuple(int(s) for s in np.cumsum(IN_SIZES)[:-1])

kernel_name = "hybrid_gmlp_gla_macaron_step"


def rmsnorm(x, g):
    xf = x.astype(jnp.float32)
    y = xf * lax.rsqrt(jnp.mean(xf * xf, axis=-1, keepdims=True) + EPS)
    return (y * g.astype(jnp.float32)).astype(x.dtype)


def layernorm(x, g, b):
    xf = x.astype(jnp.float32)
    mu = jnp.mean(xf, axis=-1, keepdims=True)
    var = jnp.mean(jnp.square(xf - mu), axis=-1, keepdims=True)
    y = (xf - mu) * lax.rsqrt(var + EPS)
    return (y * g.astype(jnp.float32) + b.astype(jnp.float32)).astype(x.dtype)


def swiglu(h, w_in, w_out):
    gate, up = jnp.split(h @ w_in, 2, axis=-1)
    return (jax.nn.silu(gate) * up) @ w_out


def gla_recurrence(q, k, v, log_a, s0, block):
    n, t, h, _ = q.shape
    dv = v.shape[-1]
    nb = t // block

    def to_blocks(a):
        return jnp.moveaxis(a.astype(jnp.float32).reshape(n, nb, block, h, a.shape[-1]), 1, 0)

    causal = jnp.tril(jnp.ones((block, block), dtype=bool))

    def step(s, blk):
        qb, kb, vb, la = blk
        b = jnp.cumsum(la, axis=1)
        b_last = b[:, -1]
        q_dec = qb * jnp.exp(b)
        scores = jnp.einsum('nthk,nshk->nhts', q_dec, kb * jnp.exp(-b))
        scores = jnp.where(causal, scores, 0.0)
        o = jnp.einsum('nhts,nshv->nthv', scores, vb) + jnp.einsum('nthk,nhkv->nthv', q_dec, s)
        k_state = kb * jnp.exp(b_last[:, None] - b)
        s = s * jnp.exp(b_last)[..., None] + jnp.einsum('nshk,nshv->nhkv', k_state, vb)
        return s, o

    s_fin, os = lax.scan(step, s0.astype(jnp.float32), tuple(map(to_blocks, (q, k, v, log_a))))
    o = jnp.moveaxis(os, 0, 1).reshape(n, t, h, dv)
    return o, s_fin


def layer(x, p, s0, pos, chunk_len, gla_block,
          g_ffn1, w_ffn1_in, w_ffn1_out, g_mix, w_in, ln_v_g, ln_v_b, w_spatial, b_spatial,
          w_gate_up, b_gate, g_gla_out, w_proj_a, w_proj_b, w_out,
          g_ffn2, w_ffn2_in, w_ffn2_out, g_ple, w_ple_gate, w_ple):
    n, t, _ = x.shape
    x = x + 0.5 * swiglu(rmsnorm(x, g_ffn1), w_ffn1_in, w_ffn1_out)

    h = rmsnorm(x, g_mix)
    u_a, v_a, q_b, k_b, v_b, g_b, lr_b, gate_a, gate_b = jnp.split(h @ w_in, IN_SPLITS, axis=-1)

    u_a = jax.nn.gelu(u_a)
    v_n = layernorm(jax.nn.gelu(v_a), ln_v_g, ln_v_b)
    mask = pos[:, None] >= pos[None, :]
    ws = jnp.where(mask, w_spatial[:, pos[:, None], pos[None, :]], 0.0).astype(x.dtype)
    bias = b_spatial[:, pos].T[:, :, None].astype(x.dtype)
    vb = v_n.reshape(n, t // chunk_len, chunk_len, A_GROUPS, A_GROUP_DIM)
    mixed = jnp.einsum('gts,ncsgd->nctgd', ws, vb) + bias
    y_a = u_a * mixed.reshape(n, t, A_WIDTH)

    q = q_b.reshape(n, t, B_HEADS, B_DK) * (B_DK ** -0.5)
    k = k_b.reshape(n, t, B_HEADS, B_DK)
    v = v_b.reshape(n, t, B_HEADS, B_DV)
    logit = (lr_b @ w_gate_up + b_gate).astype(jnp.float32)
    log_a = (jax.nn.log_sigmoid(logit) / GATE_NORM).reshape(n, t, B_HEADS, B_DK)
    o, s_new = gla_recurrence(q, k, v, log_a, s0, gla_block)
    o = rmsnorm(o, g_gla_out).astype(x.dtype).reshape(n, t, B_VW)
    y_b = o * jax.nn.silu(g_b)

    mix = jax.nn.sigmoid(gate_a) * (y_a @ w_proj_a) + jax.nn.sigmoid(gate_b) * (y_b @ w_proj_b)
    x = x + mix @ w_out

    x = x + 0.5 * swiglu(rmsnorm(x, g_ffn2), w_ffn2_in, w_ffn2_out)

    x = x + (p @ w_ple) * jax.nn.sigmoid(rmsnorm(x, g_ple) @ w_ple_gate)
    return x, s_new.astype(s0.dtype), v_n


def setup_inputs(seed: int = 0) -> dict:
    key = jax.random.key(seed)
    ks = iter(jax.random.split(key, 40))

    def nrm(shape, scale):
        return jax.random.normal(next(ks), shape, jnp.float32) * scale

    def gain(shape):
        return 1.0 + nrm(shape, 0.02)

    L = DEPTH
    return {
        "x_prompt": nrm((BATCH, SEQ, D_MODEL), 1.0),
        "x_sample": nrm((DEC_BATCH, DEC_SEQ, D_MODEL), 1.0),
        "p_prompt": nrm((L, BATCH, SEQ, PLE_DIM), 1.0),
        "p_sample": nrm((L, DEC_BATCH, DEC_SEQ, PLE_DIM), 1.0),
        "state_gla": nrm((L, DEC_BATCH, B_HEADS, B_DK, B_DV), 0.1),
        "g_ffn1": gain((L, D_MODEL)),
        "w_ffn1_in": nrm((L, D_MODEL, 2 * D_FF), D_MODEL ** -0.5),
        "w_ffn1_out": nrm((L, D_FF, D_MODEL), D_FF ** -0.5),
        "g_mix": gain((L, D_MODEL)),
        "w_in": nrm((L, D_MODEL, IN_COLS), D_MODEL ** -0.5),
        "ln_v_g": gain((L, A_WIDTH)),
        "ln_v_b": nrm((L, A_WIDTH), 0.02),
        "w_spatial": nrm((L, A_GROUPS, CHUNK, CHUNK), CHUNK ** -0.5),
        "b_spatial": 1.0 + nrm((L, A_GROUPS, CHUNK), 0.02),
        "w_gate_up": nrm((L, GATE_RANK, B_KW), GATE_RANK ** -0.5),
        "b_gate": nrm((L, B_KW), 0.02),
        "g_gla_out": gain((L, B_DV)),
        "w_proj_a": nrm((L, A_WIDTH, D_MODEL), A_WIDTH ** -0.5),
        "w_proj_b": nrm((L, B_VW, D_MODEL), B_VW ** -0.5),
        "w_out": nrm((L, D_MODEL, D_MODEL), D_MODEL ** -0.5),
        "g_ffn2": gain((L, D_MODEL)),
        "w_ffn2_in": nrm((L, D_MODEL, 2 * D_FF), D_MODEL ** -0.5),
        "w_ffn2_out": nrm((L, D_FF, D_MODEL), D_FF ** -0.5),
        "g_ple": gain((L, D_MODEL)),
        "w_ple_gate": nrm((L, D_MODEL, D_MODEL), D_MODEL ** -0.5),
        "w_ple": nrm((L, PLE_DIM, D_MODEL), PLE_DIM ** -0.5),
        "g_final": gain((D_MODEL,)),
    }


def reference(x_prompt, x_sample, p_prompt, p_sample, state_gla,
              g_ffn1, w_ffn1_in, w_ffn1_out, g_mix, w_in, ln_v_g, ln_v_b, w_spatial, b_spatial,
              w_gate_up, b_gate, g_gla_out, w_proj_a, w_proj_b, w_out,
              g_ffn2, w_ffn2_in, w_ffn2_out, g_ple, w_ple_gate, w_ple, g_final):
    pos_prompt = np.arange(CHUNK)
    pos_sample = PAST_LEN % CHUNK + np.arange(DEC_SEQ)
    s0_prompt = jnp.zeros((BATCH, B_HEADS, B_DK, B_DV), x_prompt.dtype)
    xp, xs = x_prompt, x_sample
    sp_list, ss_list, vs_list = [], [], []
    for i in range(DEPTH):
        w = (g_ffn1[i], w_ffn1_in[i], w_ffn1_out[i], g_mix[i], w_in[i], ln_v_g[i], ln_v_b[i],
             w_spatial[i], b_spatial[i], w_gate_up[i], b_gate[i], g_gla_out[i], w_proj_a[i],
             w_proj_b[i], w_out[i], g_ffn2[i], w_ffn2_in[i], w_ffn2_out[i], g_ple[i],
             w_ple_gate[i], w_ple[i])
        xp, sp, _ = layer(xp, p_prompt[i], s0_prompt, pos_prompt, CHUNK, GLA_BLOCK, *w)
        xs, ss, vs = layer(xs, p_sample[i], state_gla[i], pos_sample, DEC_SEQ, DEC_SEQ, *w)
        sp_list.append(sp)
        ss_list.append(ss)
        vs_list.append(vs)
    y_prompt = rmsnorm(xp, g_final)
    y_sample = rmsnorm(xs, g_final)
    state_gla_prompt = jnp.stack(sp_list)
    state_gla_sample = jnp.stack(ss_list)
    chunk_v_sample = jnp.stack(vs_list)
    return (y_prompt, y_sample, state_gla_prompt, state_gla_sample, chunk_v_sample)
```

```python
import math
from contextlib import ExitStack

import numpy as np
import concourse.bass as bass
import concourse.mybir as mybir
from concourse.bass_utils import run_bass_kernel_spmd

F32 = mybir.dt.float32
BF16 = mybir.dt.bfloat16
AF = mybir.ActivationFunctionType
ALU = mybir.AluOpType

ENGS = ("pe", "act", "dve", "pool", "sp")
EPS = 1e-6
NCORES = 8
D = 1024
KC = 8
DFF = 2816
JH = 22
PT = 512
NT = 4
NS = 16
TT = PT + NS
NTOK = 2048 + NS
SLOT = 4096
NSLOT = 5


class _FirstMatmul:
    def __init__(self, eng):
        self._e = eng
        self.first = None

    def matmul(self, *a, **k):
        ins = self._e.matmul(*a, **k)
        if self.first is None:
            self.first = ins
        return ins

    def __getattr__(self, name):
        return getattr(self._e, name)


class Prog:
    def __init__(self, nc):
        self.nc = nc
        self.ops = []
        self.lastw = {}
        self.readers = {}
        self.fence_deps = {}

    def fence(self, region):
        key = "OVuse" + region
        deps = set(self.readers.get(key, ()))
        self.fence_deps[region] = self._prune(deps)
        self.readers[key] = []

    def _prune(self, deps):
        best = {}
        keep = set()
        for d in deps:
            o = self.ops[d]
            if o["dma_sem"] is not None:
                keep.add(d)
            else:
                e = o["eng"]
                if e not in best or best[e] < d:
                    best[e] = d
        keep.update(best.values())
        return keep

    def add(self, eng, fn, reads=(), writes=(), dma_sem=None, extra_deps=(), ov=False):
        idx = len(self.ops)
        deps = set(extra_deps)
        reads = list(reads)
        if ov:
            for region in ov:
                reads.append("OVuse" + region)
                deps.update(self.fence_deps.get(region, ()))
        psdeps = set()
        for k in reads:
            w = self.lastw.get(k)
            if w is not None:
                deps.add(w)
        for k in writes:
            isps = isinstance(k, tuple) and k[0] == "ps"
            tgt = psdeps if isps else deps
            w = self.lastw.get(k)
            if w is not None:
                tgt.add(w)
            tgt.update(self.readers.get(k, ()))
        psdeps -= deps
        deps |= psdeps
        for k in reads:
            self.readers.setdefault(k, []).append(idx)
        for k in writes:
            self.lastw[k] = idx
            self.readers[k] = []
        keep = self._prune(deps)
        if eng == "pe":
            keep = {d for d in keep if not (self.ops[d]["eng"] == "pe" and self.ops[d]["dma_sem"] is None)}
        self.ops.append(dict(eng=eng, fn=fn, deps=keep, dma_sem=dma_sem, psdeps=(psdeps & keep)))
        return idx

    def emit(self, stack):
        nc = self.nc
        ops = self.ops
        for o in ops:
            o["target"] = False
        for o in ops:
            for d in o["deps"]:
                ops[d]["target"] = True
        cnt = {}
        semnames = set()
        for o in ops:
            if o["dma_sem"] is not None:
                s = "d_" + o["dma_sem"]
                cnt[s] = cnt.get(s, 0) + 16
                o["semval"] = (s, cnt[s])
                semnames.add(s)
            elif o["target"]:
                s = "e_" + o["eng"]
                cnt[s] = cnt.get(s, 0) + 1
                o["semval"] = (s, cnt[s])
                semnames.add(s)
        sems = {s: stack.enter_context(nc.semaphore(s)) for s in sorted(semnames)}
        self.nsems = len(sems)
        block = stack.enter_context(nc.Block())
        per_eng = {e: [o for o in ops if o["eng"] == e] for e in ENGS}

        def make(ename):
            def body(eng):
                seen = {}
                for o in per_eng[ename]:
                    dl = sorted(((ops[d]["semval"], d) for d in o["deps"]), key=lambda t: -t[0][1])
                    need = []
                    for (s, v), d in dl:
                        if seen.get(s, 0) >= v:
                            continue
                        need.append((s, v, d))
                        seen[s] = v
                    attach = max(need, key=lambda t3: t3[2]) if need else None
                    for t3 in need:
                        if t3 is not attach:
                            eng.wait_ge(sems[t3[0]], t3[1])
                    if attach is not None and ename == "pe":
                        rec = _FirstMatmul(eng)
                        ins = o["fn"](rec)
                        rec.first._wait_ge(sems[attach[0]], attach[1])
                    else:
                        ins = o["fn"](eng)
                        if attach is not None:
                            ins._wait_ge(sems[attach[0]], attach[1])
                    if o["dma_sem"] is not None:
                        ins.then_inc(sems[o["semval"][0]], 16)
                    elif o["target"]:
                        ins.then_inc(sems[o["semval"][0]], 1)
            return body

        block.tensor(make("pe"))
        block.scalar(make("act"))
        block.vector(make("dve"))
        block.gpsimd(make("pool"))
        block.sync(make("sp"))


C_GFFN1, C_GMIX, C_GFFN2, C_GPLE, C_GFIN, C_GGLA, NCOLS = 0, 8, 16, 24, 32, 40, 42


def build_program(debug_taps=()):
    nc = bass.Bass("TRN2", target_bir_lowering=False)

    def din(name, shape):
        return nc.dram_tensor(name, list(shape), F32, kind="ExternalInput").ap()

    def dout(name, shape):
        return nc.dram_tensor(name, list(shape), F32, kind="ExternalOutput").ap()

    xT_d = din("xT", [D, NTOK])
    pT_d = din("pT", [256, NTOK])
    st_d = din("st", [NS, 4, 128, 256])
    cols_d = din("cols", [128, NCOLS])
    w1i_d = din("w_ffn1_in", [D, 2 * DFF])
    w1o_d = din("w_ffn1_out", [DFF, D])
    win_d = din("w_in", [D, 6160])
    lng_d = din("ln_v_g", [512])
    lnb_d = din("ln_v_b", [512])
    wsT_d = din("wsT", [4, 128, 128])
    wsp_d = din("w_spatial", [4, 128, 128])
    bsp_d = din("b_spatial", [4, 128])
    wgu_d = din("w_gate_up", [16, 512])
    bg_d = din("b_gate", [512])
    wpa_d = din("w_proj_a", [512, D])
    wpb_d = din("w_proj_b", [D, D])
    wout_d = din("w_out", [D, D])
    w2i_d = din("w_ffn2_in", [D, 2 * DFF])
    w2o_d = din("w_ffn2_out", [DFF, D])
    wpg_d = din("w_ple_gate", [D, D])
    wple_d = din("w_ple", [256, D])

    yT_d = dout("yT", [D, NTOK])
    sp_d = dout("sp_out", [4, 128, 256])
    ss_d = dout("ss_out", [NS, 4, 128, 256])
    cv_d = dout("cv_out", [NS, 512])
    dbg_d = {n: dout("dbg_" + n, shp) for n, shp in debug_taps}

    st = ExitStack()
    with st:
        def sb(name, shape, dt):
            return st.enter_context(nc.sbuf_tensor("s_" + name, list(shape), dt))

        p = Prog(nc)

        xTb = [sb("xT0", [128, KC, TT], F32), sb("xT1", [128, KC, TT], F32)]
        xn = sb("xn", [128, KC, TT], BF16)
        rinv = sb("rinv", [128, TT], F32)
        ring = [sb(f"ring{i}", [128, SLOT], BF16) for i in range(NSLOT)]
        tmps = [sb(f"tmp{i}", [128, 512], F32) for i in range(4)]
        OVA, OVB = 6272, 11656
        ova = sb("ova", [128, OVA], F32)
        ovb_ = sb("ovb", [128, OVB], F32)
        psum = [st.enter_context(nc.psum_tensor(f"ps{i}", [128, 512], F32)) for i in range(8)]

        h_v = ova[:, 0:5808].bitcast(BF16).rearrange("p (j t) -> p j t", j=JH)
        wo_v = ovb_[:, 0:11264].bitcast(BF16).rearrange("p (j n) -> p j n", j=JH)
        ma = ova[:, 0:2112].bitcast(BF16).rearrange("p (c t) -> p c t", c=8)
        sgb = ova[:, 2112:4224].bitcast(BF16).rearrange("p (c t) -> p c t", c=8)
        s0b = [ova[:, 4224:5248].rearrange("p (h v) -> p h v", h=4),
               ova[:, 5248:6272].rearrange("p (h v) -> p h v", h=4),
               ovb_[:, 0:1024].rearrange("p (h v) -> p h v", h=4),
               ovb_[:, 1024:2048].rearrange("p (h v) -> p h v", h=4)]
        s0reg = ["A", "A", "1", "1"]
        _o = [0]

        def ovb(nwords, shape_str=None, **kw):
            a = _o[0]
            _o[0] += nwords
            assert _o[0] <= OVB, _o[0]
            v = ovb_[:, a:a + nwords].bitcast(BF16)
            if shape_str:
                v = v.rearrange(shape_str, **kw)
            return v

        uT = ovb(1056, "p (c t) -> p c t", c=4)
        vn = ovb(1280, "p (b f) -> p b f", b=5)
        ETv = ovb(1056, "p (c t) -> p c t", c=4)
        EIv = ovb(1056, "p (c t) -> p c t", c=4)
        Krev = ovb(1024, "p (b f) -> p b f", b=4)
        qdT = ovb(1056, "p (c t) -> p c t", c=4)
        kdT = ovb(1056, "p (c t) -> p c t", c=4)
        kst = ovb(1024, "p (b f) -> p b f", b=4)
        vt = ovb(2048, "p (b f) -> p b f", b=4)

        cvf = sb("cvf", [16, 512], F32)
        lrT = sb("lrT", [17, TT], BF16)
        lp = [sb("lp0", [128, 512], F32), sb("lp1", [128, 512], F32)]
        alast = sb("alast", [128, 4, 4], F32)
        ks_s = sb("ks_s", [16, 512], BF16)
        vs_s = sb("vs_s", [16, 1024], BF16)
        PTb = [sb("PT0", [128, 512], BF16), sb("PT1", [128, 512], BF16)]
        sqb = [sb("sq0", [128, 1024], BF16), sb("sq1", [128, 1024], BF16)]
        Sf = sb("Sf", [128, 4, 256], F32)
        Sbs = [sb("Sb0", [128, 4, 256], BF16), sb("Sb1", [128, 4, 256], BF16)]
        snb = [sb("snb0", [128, 4, 256], BF16), sb("snb1", [128, 4, 256], BF16)]
        kmask = [sb("km0", [16, 512], BF16), sb("km1", [16, 512], BF16)]
        aTs = sb("aTs", [128, 64], F32)
        lpTs = sb("lpTs", [128, 64], F32)
        stat6 = sb("stat6", [128, 6], F32)
        rstd4 = sb("rstd4", [128, 5], F32)
        mv4 = sb("mv4", [128, 5, 2], F32)
        pTbs = [sb("pTb0", [128, 2, TT], BF16), sb("pTb1", [128, 2, TT], BF16)]
        cols = sb("cols", [128, NCOLS], F32)
        maskT = sb("maskT", [128, 4, 128], BF16)
        lng = sb("lng", [128, 512], F32)
        lnb = sb("lnb", [128, 512], F32)
        Uneg = sb("Uneg", [128, 128], F32)
        Lneg = sb("Lneg", [128, 128], F32)
        ones_b = sb("ones_b", [128, 128], BF16)
        onesf = sb("onesf", [128, 128], F32)
        ident = sb("ident", [128, 128], F32)
        WsT = sb("WsT", [128, 4, 128], BF16)
        brow = sb("brow", [1, 4, 128], BF16)
        bcol = sb("bcol", [128, 4], F32)
        w00 = sb("w00", [128, 4], F32)
        Dg = sb("Dg", [16, 4, 16], BF16)
        wgu = sb("wgu", [17, 512], BF16)
        wlr = sb("wlr", [128, KC, 16], BF16)
        dummy = sb("dummy", [1, 2], F32)

        psc = [0]
        ps_reserved = set()

        def newps():
            assert len(ps_reserved) < 8, "all PSUM banks reserved"
            while True:
                i = psc[0] % 8
                psc[0] += 1
                if i not in ps_reserved:
                    return i

        tmc = [0]

        def newtmp():
            i = tmc[0] % 4
            tmc[0] += 1
            return i

        slc = [0]

        def kk(name, idxs, blks):
            return [(name, i, b) for i in idxs for b in blks]

        def mm(out, pairs, reads, writes, ov=False, pair_reads=None):
            if pair_reads is not None:
                n = len(pairs)
                last = None
                for i, (l, r) in enumerate(pairs):
                    def fn1(e, i=i, l=l, r=r):
                        return e.matmul(out, l, r, start=(i == 0), stop=(i == n - 1))
                    last = p.add("pe", fn1, list(reads) + list(pair_reads[i]), writes, ov=ov)
                return last

            def fn(e):
                n = len(pairs)
                ins = None
                for i, (l, r) in enumerate(pairs):
                    ins = e.matmul(out, l, r, start=(i == 0), stop=(i == n - 1))
                return ins
            return p.add("pe", fn, reads, writes, ov=ov)

        def op(eng, f, reads, writes, ov=False):
            return p.add(eng, f, reads, writes, ov=ov)

        def load_piece(src3, kc, ncols):
            s = slc[0] % NSLOT
            slc[0] += 1
            view = ring[s][:, 0:kc * ncols].rearrange("p (k n) -> p k n", k=kc)
            p.add("pool", lambda e: e.dma_start(out=view, in_=src3), writes=[("slot", s)], dma_sem=f"slot{s}")
            return view, ("slot", s)

        def wview(w_d, kc, c0, ncols, r0=0):
            return w_d.rearrange("(k p) n -> p k n", p=128)[:, r0:r0 + kc, c0:c0 + ncols]

        xv = xT_d.rearrange("(k p) n -> p k n", p=128)
        p.add("sp", lambda e: e.dma_start(out=xTb[0][:, :, 0:PT], in_=xv[:, :, 0:PT]),
              writes=kk(("xT", 0), range(KC), [0, 1, 2, 3]), dma_sem="xl0")
        p.add("sp", lambda e: e.dma_start(out=xTb[0][:, :, PT:TT], in_=xv[:, :, 2048:NTOK]),
              writes=kk(("xT", 0), range(KC), [4]), dma_sem="xls")
        pre0 = (load_piece(wview(w1i_d, KC, 0, 512), KC, 512), load_piece(wview(w1i_d, KC, DFF, 512), KC, 512))

        cdma = []

        def cload(eng, out, in_, slow=False):
            cdma.append(p.add(eng, lambda e: e.dma_start(out=out, in_=in_, allow_slow_non_contiguous=slow), dma_sem="c_" + eng))

        p.add("sp", lambda e: e.dma_start(out=cols[:], in_=cols_d), writes=["c_cols"], dma_sem="ccols")
        cload("pool", lng[:], lng_d.partition_broadcast(128))
        cload("pool", lnb[:], lnb_d.partition_broadcast(128))
        wsf = lp[0][:, :].rearrange("p (g t) -> p g t", g=4)
        cload("pool", wsf, wsT_d.rearrange("g s t -> s g t"))
        cload("pool", bcol[:], bsp_d[:, 0].partition_broadcast(128), slow=True)
        cload("pool", w00[:], wsp_d[:, 0, 0].partition_broadcast(128), slow=True)
        cload("pool", brow[:], bsp_d.rearrange("(o g) t -> o g t", o=1))
        cload("pool", wgu[0:16, :], wgu_d)
        cload("pool", wgu[16:17, :], bg_d.rearrange("(o n) -> o n", o=1))
        cload("pool", wlr[:], wview(win_d, KC, 4096, 16))
        CK = ["c_ln", "c_wsf", "c_bcol", "c_w00", "c_brow", "c_wgu", "c_wlr"]
        p.add("sp", lambda e: e.nop(), writes=CK, extra_deps=cdma)

        op("dve", lambda e: e.memset(onesf[:], 1.0), [], ["onesf"])
        op("dve", lambda e: e.memset(ones_b[:], 1.0), [], ["ones_b"])
        op("dve", lambda e: e.memset(lrT[:], 1.0), [], ["lrT_init"])
        op("dve", lambda e: e.memset(mv4[:], 1.0), [], ["mv4_init"])
        op("pool", lambda e: e.memset(Uneg[:], -1.0 / 16.0), [], ["Uneg"])
        op("pool", lambda e: e.memset(Lneg[:], -1.0 / 16.0), [], ["Lneg"])
        op("pool", lambda e: e.affine_select(Uneg[:], Uneg[:], [[1, 128]], ALU.is_ge, 0.0, base=0, channel_multiplier=-1), ["Uneg"], ["Uneg"])
        op("pool", lambda e: e.affine_select(Lneg[:], Lneg[:], [[-1, 128]], ALU.is_gt, 0.0, base=0, channel_multiplier=1), ["Lneg"], ["Lneg"])
        op("pool", lambda e: e.affine_select(ident[:], onesf[:], [[1, 128]], ALU.is_equal, 0.0, base=0, channel_multiplier=-1), ["onesf"], ["ident"])
        op("pool", lambda e: e.affine_select(maskT[:], onesf[:].unsqueeze(1).to_broadcast([128, 4, 128]), [[0, 4], [1, 128]], ALU.is_ge, 0.0, base=0, channel_multiplier=-1), ["onesf"], ["maskT"])
        op("pool", lambda e: e.affine_select(WsT[:], wsf, [[0, 4], [1, 128]], ALU.is_ge, 0.0, base=0, channel_multiplier=-1), ["c_wsf", ("lp", 0)], ["WsT"])

        def tile_info(t):
            if t == 0:
                return TT, [(0, 256, [0, 1]), (256, 272, [2, 3, 4])], [0, 1, 2, 3, 4]
            return PT, [(0, 512, [0, 1, 2, 3])], [0, 1, 2, 3]

        def blkcols(b):
            return (b * 128, 128) if b < 4 else (PT, NS)

        stat = {}

        def stats_begin(segs, delay=1):
            stat["ps"] = [newps() for _ in segs]
            ps_reserved.update(stat["ps"])
            stat["cnt"] = [0] * len(segs)
            stat["pending"] = []
            stat["delay"] = delay

        def stats_feed(xT, xb, c, si, seg):
            c0, n, blks = seg
            ps = stat["ps"][si]
            ti = newtmp()
            sq = tmps[ti][:, 0:256].bitcast(BF16)[:, 0:n]
            op("act", lambda e: e.activation(sq, xT[:, c, c0:c0 + n], AF.Square), kk(("xT", xb), [c], blks), [("tmp", ti)])
            k = stat["cnt"][si]
            stat["cnt"][si] += 1
            p.add("pe", lambda e: e.matmul(psum[ps][:, 0:n], ones_b[:], sq, start=(k == 0), stop=(k == KC - 1)),
                  [("tmp", ti), "ones_b"], [("ps", ps)])

        def stats_done(xT, xb, c, si, seg):
            stat["pending"].append((xT, xb, c, si, seg))

        def stats_pump():
            while stat.get("pending"):
                stats_feed(*stat["pending"].pop(0))

        def stats_flush():
            while stat["pending"]:
                stats_feed(*stat["pending"].pop(0))

        def act_prefetch_ln():
            op("act", lambda e: e.activation(dummy[0:1, 0:1], onesf[0:1, 0:1], AF.Ln), ["onesf"], ["dummy"])

        def stats_finish(segs):
            stats_flush()
            for si, (c0, n, blks) in enumerate(segs):
                assert stat["cnt"][si] == KC
                ps = stat["ps"][si]
                ti = newtmp()
                lt = tmps[ti][:, 0:n]
                op("act", (lambda lt, ps, n: lambda e: e.activation(lt, psum[ps][:, 0:n], AF.Ln, bias=EPS, scale=1.0 / D))(lt, ps, n),
                   [("ps", ps)], [("tmp", ti)])
                op("act", (lambda lt, c0, n: lambda e: e.activation(rinv[:, c0:c0 + n], lt, AF.Exp, scale=-0.5))(lt, c0, n),
                   [("tmp", ti)], kk("rinv", [0], blks))
            ps_reserved.difference_update(stat["ps"])

        def rmsnorm_to_xn(xT, xb, segs, gc):
            stats_finish(segs)
            for (c0, n, blks) in segs:
                for kc in range(KC):
                    op("dve", (lambda kc, c0, n: lambda e: e.scalar_tensor_tensor(
                        xn[:, kc, c0:c0 + n], xT[:, kc, c0:c0 + n], cols[:, gc + kc:gc + kc + 1], rinv[:, c0:c0 + n],
                        ALU.mult, ALU.mult))(kc, c0, n),
                       kk(("xT", xb), [kc], blks) + kk("rinv", [0], blks) + ["c_cols"], kk("xn", [kc], blks))

        def proj_fm(piece, pkey, off, rhs_buf, rkey, nkc, seg, ps, ov=False):
            c0, n, blks = seg
            pairs = [(piece[:, kc, off:off + 128], rhs_buf[:, kc, c0:c0 + n]) for kc in range(nkc)]
            mm(psum[ps][:, 0:n], pairs, [pkey], [("ps", ps)], ov=ov, pair_reads=[kk(rkey, [kc], blks) for kc in range(nkc)])

        def load_wo(w_out_d, which=range(6)):
            wo_view = w_out_d.rearrange("(j p) n -> p j n", p=128)
            for i in which:
                j0 = 4 * i
                nj = min(4, JH - j0)
                p.add("pool", (lambda j0, nj: lambda e: e.dma_start(out=wo_v[:, j0:j0 + nj, :], in_=wo_view[:, j0:j0 + nj, :]))(j0, nj),
                      writes=[("wo", i)], dma_sem=f"wo{i}", ov=("1" if i < 2 else ("12" if i == 2 else "2")))

        def ffn_pieces(w_in_d, G):
            j0 = 4 * G
            nj = min(4, JH - j0)
            return (load_piece(wview(w_in_d, KC, j0 * 128, nj * 128), KC, nj * 128),
                    load_piece(wview(w_in_d, KC, DFF + j0 * 128, nj * 128), KC, nj * 128))

        def proj_fm_multi(items, rhs_buf, rkey, nkc, seg):
            c0, n, blks = seg
            last = len(items) - 1
            for kc in range(nkc):
                for (piece, pkey, off, ps) in items:
                    def fn1(e, piece=piece, off=off, ps=ps, kc=kc):
                        return e.matmul(psum[ps][:, 0:n], piece[:, kc, off:off + 128], rhs_buf[:, kc, c0:c0 + n], start=(kc == 0), stop=(kc == nkc - 1))
                    p.add("pe", fn1, [pkey] + kk(rkey, [kc], blks), [("ps", ps)])

        def ffn(xT, xb, segs, w_in_d, gc, pre, wo_late=None):
            rmsnorm_to_xn(xT, xb, segs, gc)
            for G in range(6):
                j0 = 4 * G
                nj = min(4, JH - j0)
                (pg, kg), (pu, ku) = pre if G == 0 else ffn_pieces(w_in_d, G)
                if wo_late is not None and G >= 1:
                    load_wo(wo_late, [G - 1] if G < 5 else [4, 5])
                first = {}
                if G == 0:
                    for seg in segs:
                        items = []
                        for jj in range(2):
                            psg, psu = newps(), newps()
                            ps_reserved.update((psg, psu))
                            first[(jj, seg[0])] = (psg, psu)
                            items += [(pg, kg, jj * 128, psg), (pu, ku, jj * 128, psu)]
                        proj_fm_multi(items, xn, "xn", KC, seg)
                for jj in range(nj):
                    j = j0 + jj
                    for seg in segs:
                        c0, n, blks = seg
                        if (jj, c0) in first:
                            psg, psu = first[(jj, c0)]
                            ps_reserved.difference_update((psg, psu))
                        else:
                            psg = newps()
                            proj_fm(pg, kg, jj * 128, xn, "xn", KC, seg, psg)
                            psu = newps()
                            proj_fm(pu, ku, jj * 128, xn, "xn", KC, seg, psu)
                        ti = newtmp()
                        sg = tmps[ti][:, 0:n]
                        op("act", (lambda sg, psg, n: lambda e: e.activation(sg, psum[psg][:, 0:n], AF.Silu))(sg, psg, n),
                           [("ps", psg)], [("tmp", ti)])
                        op("dve", (lambda sg, psu, j, c0, n: lambda e: e.tensor_tensor(h_v[:, j, c0:c0 + n], sg, psum[psu][:, 0:n], ALU.mult))(sg, psu, j, c0, n),
                           [("tmp", ti), ("ps", psu)], kk("h", [j], blks), ov="A")
            act_prefetch_ln()
            stats_begin(segs)
            for c in range(KC):
                for si, seg in enumerate(segs):
                    c0, n, blks = seg
                    ps = newps()
                    pairs = [(wo_v[:, j, c * 128:(c + 1) * 128], h_v[:, j, c0:c0 + n]) for j in range(JH)]
                    mm(psum[ps][:, 0:n], pairs, [("wo", i) for i in range(6)] + kk("h", range(JH), blks), [("ps", ps)], ov="A12")
                    stats_pump()
                    op("dve", (lambda c, c0, n, ps: lambda e: e.scalar_tensor_tensor(
                        xT[:, c, c0:c0 + n], psum[ps][:, 0:n], 0.5, xT[:, c, c0:c0 + n], ALU.mult, ALU.add))(c, c0, n, ps),
                       [("ps", ps)] + kk(("xT", xb), [c], blks), kk(("xT", xb), [c], blks))
                    stats_done(xT, xb, c, si, seg)

        def gated_proj(xT, xb, segs, gate_pieces, w2_pieces, nkc2, rhs2, rkey2, final, segmajor=False, hook=None, mid=None, last_prefetch=False):
            if segmajor:
                order = [(c, si) for si in range(len(segs)) for c in range(KC)]
            else:
                order = [(c, si) for c in range(KC) for si in range(len(segs))]
            for oi, (c, si) in enumerate(order):
                if segmajor and si == 1 and c == 0 and mid is not None:
                    mid()
                gp, gk = gate_pieces[c // 4]
                wp, wk = w2_pieces[(c // 4) if len(w2_pieces) > 1 else 0]
                off2 = (c % 4) * 128 if len(w2_pieces) > 1 else c * 128
                if True:
                    seg = segs[si]
                    c0, n, blks = seg
                    ps1 = newps()
                    proj_fm(gp, gk, (c % 4) * 128, xn, "xn", KC, seg, ps1)
                    ps2 = newps()
                    proj_fm(wp, wk, off2, rhs2, rkey2, nkc2, seg, ps2, ov={"uT": "1", "sgb": "A"}.get(rkey2, False) if isinstance(rkey2, str) else False)
                    stats_pump()
                    ti = newtmp()
                    sg = tmps[ti][:, 0:n]
                    op("act", (lambda sg, ps1, n: lambda e: e.activation(sg, psum[ps1][:, 0:n], AF.Sigmoid))(sg, ps1, n),
                       [("ps", ps1)], [("tmp", ti)])
                    if last_prefetch and oi == len(order) - 1:
                        act_prefetch_ln()
                    final(c, seg, sg, ti, ps2, si)
                    if segmajor and si == 0 and hook is not None:
                        hook()

        def mixer(t, xT, xb, ncol, segs, blks_all, on_m11, on_m12):
            pblks = [b for b in blks_all if b < 4]
            has_s = 4 in blks_all
            p.fence("A")
            p.fence("1")
            p.fence("2")
            if has_s:
                op("dve", lambda e: e.memset(ETv[:, :, PT:TT], 128.0 ** -0.5), [], kk("ET", range(4), [4]), ov="1")
                op("dve", lambda e: e.memset(EIv[:, :, PT:TT], 1.0), [], kk("EI", range(4), [4]), ov="1")
            rmsnorm_to_xn(xT, xb, segs, C_GMIX)

            pc, pk = load_piece(wview(win_d, KC, 0, 512), KC, 512)
            m1ps = {}
            for seg in segs:
                items = []
                for c in range(4):
                    m1ps[(c, seg[0])] = newps()
                    ps_reserved.add(m1ps[(c, seg[0])])
                    items.append((pc, pk, c * 128, m1ps[(c, seg[0])]))
                proj_fm_multi(items, xn, "xn", KC, seg)
            for c in range(4):
                for seg in segs:
                    c0, n, blks = seg
                    ps = m1ps[(c, c0)]
                    ps_reserved.discard(ps)
                    op("act", (lambda c, c0, n, ps: lambda e: e.activation(uT[:, c, c0:c0 + n], psum[ps][:, 0:n], AF.Gelu_apprx_tanh))(c, c0, n, ps),
                       [("ps", ps)], kk("uT", [c], blks), ov="1")

            pc, pk = load_piece(wview(win_d, KC, 512, 512), KC, 512)
            psl = {}
            for b in blks_all:
                bc0, m = blkcols(b)
                ps = newps()
                ps_reserved.add(ps)
                psl[b] = ps
                pairs = [(xn[:, kc, bc0:bc0 + m], pc[:, kc, :]) for kc in range(KC)]
                mm(psum[ps][0:m, :], pairs, [pk] + kk("xn", range(KC), [b]), [("ps", ps)])
                op("act", (lambda ps, m: lambda e: e.activation(psum[ps][0:m, :], psum[ps][0:m, :], AF.Gelu_apprx_tanh))(ps, m),
                   [("ps", ps)], [("ps", ps)])
                op("dve", (lambda ps, m: lambda e: e.bn_stats(stat6[0:m, :], psum[ps][0:m, :]))(ps, m), [("ps", ps)], ["stat6"])
                op("dve", (lambda b, m: lambda e: e.bn_aggr(mv4[0:m, b, :], stat6[0:m, :]))(b, m), ["stat6", "mv4_init"], [("mv4", b)])
            nb = len(blks_all)
            op("act", (lambda nb: lambda e: e.activation(rstd4[:, 0:nb], mv4[:, 0:nb, 1], AF.Ln, bias=EPS))(nb),
               [("mv4", b) for b in blks_all], ["rstd4"])
            op("act", (lambda nb: lambda e: e.activation(rstd4[:, 0:nb], rstd4[:, 0:nb], AF.Exp, scale=-0.5))(nb), ["rstd4"], ["rstd4"])
            for b in blks_all:
                bc0, m = blkcols(b)
                ps = psl[b]
                t2 = newtmp()
                ta = tmps[t2][0:m, :]
                op("dve", (lambda ps, ta, b, m: lambda e: e.scalar_tensor_tensor(ta, psum[ps][0:m, :], mv4[0:m, b, 0:1], lng[0:m, :], ALU.subtract, ALU.mult))(ps, ta, b, m),
                   [("ps", ps), ("mv4", b), "c_ln"], [("tmp", t2)])
                if b < 4:
                    op("dve", (lambda ta, b, m: lambda e: e.scalar_tensor_tensor(vn[0:m, b, :], ta, rstd4[0:m, b:b + 1], lnb[0:m, :], ALU.mult, ALU.add))(ta, b, m),
                       [("tmp", t2), "rstd4", "c_ln"], [("vn", b)], ov="1")
                else:
                    op("dve", (lambda ta, b, m: lambda e: e.scalar_tensor_tensor(cvf[0:m, :], ta, rstd4[0:m, b:b + 1], lnb[0:m, :], ALU.mult, ALU.add))(ta, b, m),
                       [("tmp", t2), "rstd4", "c_ln"], ["cvf"])
                    op("dve", lambda e: e.tensor_copy(vn[0:NS, 4, :], cvf[:]), ["cvf"], [("vn", 4)], ov="1")
                    p.add("sp", lambda e: e.dma_start(out=cv_d, in_=cvf[:]), reads=["cvf"], dma_sem="cv")
            ps_reserved.difference_update(psl.values())

            for seg in segs:
                c0, n, blks = seg
                ps = newps()
                pairs = [(wlr[:, kc, :], xn[:, kc, c0:c0 + n]) for kc in range(KC)]
                mm(psum[ps][0:16, 0:n], pairs, ["c_wlr"] + kk("xn", range(KC), blks), [("ps", ps)])
                op("act", (lambda ps, c0, n: lambda e: e.activation(lrT[0:16, c0:c0 + n], psum[ps][0:16, 0:n], AF.Copy))(ps, c0, n),
                   [("ps", ps), "lrT_init"], kk("lrT", [0], blks))
            pv2 = [load_piece(wview(win_d, KC, 2048 + i * 512, 512), KC, 512) for i in range(2)]

            def v_block(b):
                bc0, m = blkcols(b)
                for hv in range(2):
                    pc, pk = pv2[hv]
                    ps = newps()
                    pairs = [(xn[:, kc, bc0:bc0 + m], pc[:, kc, :]) for kc in range(KC)]
                    mm(psum[ps][0:m, :], pairs, [pk] + kk("xn", range(KC), [b]), [("ps", ps)])
                    if b < 4:
                        op("dve", (lambda b, hv, ps: lambda e: e.tensor_copy(vt[:, b, hv * 512:(hv + 1) * 512], psum[ps][:, :]))(b, hv, ps),
                           [("ps", ps)], [("vt", b, hv)], ov="2")
                    else:
                        op("dve", (lambda hv, ps: lambda e: e.tensor_copy(vs_s[:, hv * 512:(hv + 1) * 512], psum[ps][0:NS, :]))(hv, ps),
                           [("ps", ps)], [("vs_s", hv)])

            for b in pblks:
                bc0, m = blkcols(b)
                ps = newps()
                mm(psum[ps][:, :], [(lrT[0:17, bc0:bc0 + 128], wgu[0:17, :])], kk("lrT", [0], [b]) + ["c_wgu", "lrT_init"], [("ps", ps)])
                t1 = newtmp()
                lpb = lp[b % 2]
                op("act", (lambda t1, ps: lambda e: e.activation(tmps[t1][:, :], psum[ps][:, :], AF.Exp, scale=-1.0))(t1, ps), [("ps", ps)], [("tmp", t1)])
                op("act", (lambda t1, lpb: lambda e: e.activation(lpb[:, :], tmps[t1][:, :], AF.Ln, bias=1.0))(t1, lpb), [("tmp", t1)], [("lp", b % 2)])
                v_block(b)
                psb_ = newps()
                def fn(e, lpb=lpb, psb_=psb_):
                    ins = None
                    for hh in range(4):
                        ins = e.matmul(psum[psb_][:, hh * 128:(hh + 1) * 128], lpb[:, hh * 128:(hh + 1) * 128], Uneg[:], start=True, stop=True)
                    return ins
                p.add("pe", fn, [("lp", b % 2), "Uneg"], [("ps", psb_)])
                pv = psum[psb_][:, :].rearrange("p (g t) -> p g t", g=4)
                op("act", (lambda pv, bc0: lambda e: e.activation(ETv[:, :, bc0:bc0 + 128], pv, AF.Exp, bias=-0.5 * math.log(128.0)))(pv, bc0),
                   [("ps", psb_)], kk("ET", range(4), [b]), ov="1")
                op("act", (lambda pv, bc0: lambda e: e.activation(EIv[:, :, bc0:bc0 + 128], pv, AF.Exp, scale=-1.0))(pv, bc0),
                   [("ps", psb_)], kk("EI", range(4), [b]), ov="1")
                op("act", (lambda psb_, b: lambda e: e.activation(alast[:, b, :], psum[psb_][:, 127:512:128], AF.Exp))(psb_, b),
                   [("ps", psb_)], [("alast", b)])
                psr = newps()
                mm(psum[psr][:, :], [(Lneg[:], lpb[:, :])], [("lp", b % 2), "Lneg"], [("ps", psr)])
                op("act", (lambda psr, b: lambda e: e.activation(Krev[:, b, :], psum[psr][:, :], AF.Exp))(psr, b),
                   [("ps", psr)], [("Krev", b)], ov="1")
            if has_s:
                ps = newps()
                def fn(e, ps=ps):
                    ins = None
                    for hh in range(4):
                        ins = e.matmul(psum[ps][:, hh * 16:(hh + 1) * 16], wgu[0:17, hh * 128:(hh + 1) * 128], lrT[0:17, PT:TT], start=True, stop=True)
                    return ins
                p.add("pe", fn, kk("lrT", [0], [4]) + ["c_wgu", "lrT_init"], [("ps", ps)])
                op("act", (lambda ps: lambda e: e.activation(lpTs[:, :], psum[ps][:, 0:64], AF.Exp, scale=-1.0))(ps), [("ps", ps)], ["lpTs"])
                op("act", lambda e: e.activation(lpTs[:, :], lpTs[:, :], AF.Ln, bias=1.0), ["lpTs"], ["lpTs"])
                op("act", lambda e: e.activation(aTs[:, :], lpTs[:, :], AF.Exp, scale=-1.0 / 16.0), ["lpTs"], ["aTs"])

            if has_s:
                v_block(4)
            gbp = [load_piece(wview(win_d, KC, 3072 + i * 512, 512), KC, 512) for i in range(2)]
            for c in range(KC):
                pc, pk = gbp[c // 4]
                for seg in segs:
                    c0, n, blks = seg
                    ps = newps()
                    proj_fm(pc, pk, (c % 4) * 128, xn, "xn", KC, seg, ps)
                    op("act", (lambda c, c0, n, ps: lambda e: e.activation(sgb[:, c, c0:c0 + n], psum[ps][:, 0:n], AF.Silu))(c, c0, n, ps),
                       [("ps", ps)], kk("sgb", [c], blks), ov="A")

            for b in pblks:
                bc0, m = blkcols(b)
                ps = newps()
                def fn(e, b=b, ps=ps):
                    ins = None
                    for g in range(4):
                        e.matmul(psum[ps][:, g * 128:(g + 1) * 128], vn[:, b, g * 128:(g + 1) * 128], WsT[:, g, :], start=True, stop=False)
                        ins = e.matmul(psum[ps][:, g * 128:(g + 1) * 128], ones_b[0:1, :], brow[0:1, g, :], start=False, stop=True)
                    return ins
                p.add("pe", fn, [("vn", b), "WsT", "c_brow", "ones_b"], [("ps", ps)], ov="1")
                op("dve", (lambda ps, bc0: lambda e: e.tensor_tensor(
                    uT[:, :, bc0:bc0 + 128], psum[ps][:, :].rearrange("p (g t) -> p g t", g=4), uT[:, :, bc0:bc0 + 128], ALU.mult))(ps, bc0),
                   [("ps", ps)] + kk("uT", range(4), [b]), kk("uT", range(4), [b]), ov="1")
            if has_s:
                for g in range(4):
                    op("dve", (lambda g: lambda e: e.tensor_scalar(Dg[:, g, :], ident[0:16, 0:16], w00[0:16, g:g + 1], None, ALU.mult))(g),
                       ["ident", "c_w00"], [("Dg", g)])
                ps = newps()
                def fn(e, ps=ps):
                    ins = None
                    for g in range(4):
                        ins = e.matmul(psum[ps][:, g * 16:(g + 1) * 16], vn[0:NS, 4, g * 128:(g + 1) * 128], Dg[:, g, :], start=True, stop=True)
                    return ins
                p.add("pe", fn, [("vn", 4)] + [("Dg", g) for g in range(4)], [("ps", ps)], ov="1")
                for g in range(4):
                    op("dve", (lambda ps, g: lambda e: e.scalar_tensor_tensor(
                        uT[:, g, PT:TT], psum[ps][:, g * 16:(g + 1) * 16], bcol[:, g:g + 1], uT[:, g, PT:TT], ALU.add, ALU.mult))(ps, g),
                       [("ps", ps), "c_bcol"] + kk("uT", [g], [4]), kk("uT", [g], [4]), ov="1")

            ga = [load_piece(wview(win_d, KC, 4112 + i * 512, 512), KC, 512) for i in range(2)]
            wpa = load_piece(wview(wpa_d, 4, 0, 1024), 4, 1024)
            def fin_a(c, seg, sg, ti, ps2, si):
                c0, n, blks = seg
                op("dve", (lambda: lambda e: e.tensor_tensor(ma[:, c, c0:c0 + n], sg, psum[ps2][:, 0:n], ALU.mult))(),
                   [("tmp", ti), ("ps", ps2)], kk("ma", [c], blks), ov="A")
            gated_proj(xT, xb, segs, ga, [wpa], 4, uT, "uT", fin_a)

            act_prefetch_ln()
            pc, pk = load_piece(wview(win_d, KC, 1024, 512), KC, 512)
            for hh in range(4):
                for seg in segs:
                    c0, n, blks = seg
                    ps = newps()
                    proj_fm(pc, pk, hh * 128, xn, "xn", KC, seg, ps)
                    op("dve", (lambda hh, c0, n, ps: lambda e: e.tensor_tensor(qdT[:, hh, c0:c0 + n], psum[ps][:, 0:n], ETv[:, hh, c0:c0 + n], ALU.mult))(hh, c0, n, ps),
                       [("ps", ps)] + kk("ET", [hh], blks), kk("qdT", [hh], blks), ov="12")
            pc, pk = load_piece(wview(win_d, KC, 1536, 512), KC, 512)
            for hh in range(4):
                for seg in segs:
                    c0, n, blks = seg
                    ps = newps()
                    proj_fm(pc, pk, hh * 128, xn, "xn", KC, seg, ps)
                    op("dve", (lambda hh, c0, n, ps: lambda e: e.tensor_tensor(kdT[:, hh, c0:c0 + n], psum[ps][:, 0:n], EIv[:, hh, c0:c0 + n], ALU.mult))(hh, c0, n, ps),
                       [("ps", ps)] + kk("EI", [hh], blks), kk("kdT", [hh], blks), ov="12")
            for b in blks_all:
                bc0, m = blkcols(b)
                ps = newps()
                pairs = [(xn[:, kc, bc0:bc0 + m], pc[:, kc, :]) for kc in range(KC)]
                mm(psum[ps][0:m, :], pairs, [pk] + kk("xn", range(KC), [b]), [("ps", ps)])
                if b < 4:
                    op("dve", (lambda b, ps: lambda e: e.tensor_tensor(kst[:, b, :], psum[ps][:, :], Krev[:, b, :], ALU.mult))(b, ps),
                       [("ps", ps), ("Krev", b)], [("kst", b)], ov="12")
                else:
                    op("dve", (lambda ps: lambda e: e.tensor_copy(ks_s[:, :], psum[ps][0:NS, :]))(ps), [("ps", ps)], ["ks_s"])
            def epi_squares(pso, w, blk):
                si = blk % 2
                sq = sqb[si]
                for bk in range(2):
                    op("act", (lambda bk, sq, w: lambda e: e.activation(sq[:, bk * 4 * w:(bk + 1) * 4 * w], psum[pso[bk]][:, 0:4 * w], AF.Square))(bk, sq, w),
                       [("ps", pso[bk])], [("sq", si, bk)])

            def epi_rest(pso, w, cs0, blk):
                si = blk % 2
                sq = sqb[si]
                pss = newps()
                def fn(e, sq=sq, pss=pss, w=w):
                    ins = None
                    for hh in range(4):
                        for dvc in range(2):
                            r = (hh * 2 + dvc) * w
                            ins = e.matmul(psum[pss][:, hh * w:(hh + 1) * w], ones_b[:], sq[:, r:r + w], start=(dvc == 0), stop=(dvc == 1))
                    return ins
                p.add("pe", fn, [("sq", si, 0), ("sq", si, 1), "ones_b"], [("ps", pss)])
                t1 = newtmp()
                op("act", (lambda t1, pss, w: lambda e: e.activation(tmps[t1][:, 0:4 * w], psum[pss][:, 0:4 * w], AF.Ln, bias=EPS, scale=1.0 / 256.0))(t1, pss, w),
                   [("ps", pss)], [("tmp", t1)])
                op("act", (lambda t1, w: lambda e: e.activation(tmps[t1][:, 0:4 * w], tmps[t1][:, 0:4 * w], AF.Exp, scale=-0.5))(t1, w),
                   [("tmp", t1)], [("tmp", t1)])
                for bk in range(2):
                    t2 = newtmp()
                    rb = tmps[t1][:, bk * 2 * w:(bk + 1) * 2 * w].rearrange("p (h t) -> p h t", h=2).unsqueeze(2).to_broadcast([128, 2, 2, w])
                    op("dve", (lambda bk, t2, rb: lambda e: e.tensor_tensor(
                        tmps[t2][:, 0:4 * w].rearrange("p (h d t) -> p h d t", h=2, d=2),
                        psum[pso[bk]][:, 0:4 * w].rearrange("p (h d t) -> p h d t", h=2, d=2), rb, ALU.mult))(bk, t2, rb),
                       [("ps", pso[bk]), ("tmp", t1)], [("tmp", t2)])
                    op("dve", (lambda bk, t2: lambda e: e.tensor_tensor(
                        sgb[:, bk * 4:(bk + 1) * 4, cs0:cs0 + w], tmps[t2][:, 0:4 * w].rearrange("p (c t) -> p c t", c=4),
                        sgb[:, bk * 4:(bk + 1) * 4, cs0:cs0 + w], ALU.mult))(bk, t2),
                       [("tmp", t2)] + kk("sgb", range(bk * 4, bk * 4 + 4), [blk]), kk("sgb", range(bk * 4, bk * 4 + 4), [blk]), ov="A")

            def o_epilogue(pso, w, cs0, blk):
                epi_squares(pso, w, blk)
                epi_rest(pso, w, cs0, blk)

            def prompt_recurrence(bgstep):
                def scores_mask(b):
                    bc0 = b * 128
                    pss_ = newps()

                    def fn(e):
                        ins = None
                        for hh in range(4):
                            ins = e.matmul(psum[pss_][:, hh * 128:(hh + 1) * 128], kdT[:, hh, bc0:bc0 + 128], qdT[:, hh, bc0:bc0 + 128], start=True, stop=True)
                        return ins
                    p.add("pe", fn, kk("kdT", range(4), [b]) + kk("qdT", range(4), [b]), [("ps", pss_)], ov="2")
                    PTt = PTb[b % 2]
                    op("dve", lambda e: e.tensor_tensor(PTt[:, :], psum[pss_][:, :], maskT[:].rearrange("p g t -> p (g t)"), ALU.mult),
                       [("ps", pss_), "maskT"], [("PT", b % 2)])

                def s_chain(b):
                    gb = t * 4 + b
                    psu_ = [newps(), newps()]

                    def fn(e):
                        ins = None
                        for hh in range(4):
                            ins = e.matmul(psum[psu_[hh // 2]][:, (hh % 2) * 256:(hh % 2) * 256 + 256], kst[:, b, hh * 128:(hh + 1) * 128], vt[:, b, hh * 256:(hh + 1) * 256], start=True, stop=True)
                        return ins
                    p.add("pe", fn, [("kst", b), ("vt", b, 0), ("vt", b, 1)], [("ps", psu_[0]), ("ps", psu_[1])], ov="2")
                    for hh in range(4):
                        src = psum[psu_[hh // 2]][:, (hh % 2) * 256:(hh % 2) * 256 + 256]
                        if gb == 0:
                            op("dve", (lambda hh, src: lambda e: e.tensor_copy(Sf[:, hh, :], src))(hh, src), [("ps", psu_[hh // 2])], [("Sf", hh)])
                        else:
                            op("dve", (lambda hh, src: lambda e: e.scalar_tensor_tensor(Sf[:, hh, :], Sf[:, hh, :], alast[:, b, hh:hh + 1], src, ALU.mult, ALU.add))(hh, src),
                               [("ps", psu_[hh // 2]), ("alast", b), ("Sf", hh)], [("Sf", hh)])
                def s_copy(b):
                    gb = t * 4 + b
                    if gb < 15:
                        Snew = Sbs[gb % 2]
                        op("act", lambda e: e.activation(Snew[:].rearrange("p h v -> p (h v)"), Sf[:].rearrange("p h v -> p (h v)"), AF.Copy),
                           [("Sf", hh) for hh in range(4)], [("Sb", gb % 2)])
                    else:
                        p.add("sp", lambda e: e.dma_start(out=sp_d.rearrange("h k v -> k h v"), in_=Sf[:]), reads=[("Sf", hh) for hh in range(4)], dma_sem="spo")

                s_chain(pblks[0])
                s_copy(pblks[0])
                scores_mask(pblks[0])
                for bi, b in enumerate(pblks):
                    gb = t * 4 + b
                    bc0 = b * 128
                    PTt = PTb[b % 2]
                    Sprev = Sbs[(gb - 1) % 2]
                    pso = [newps(), newps()]
                    ps_reserved.update(pso)

                    def fn(e, pso=pso, PTt=PTt, b=b, bc0=bc0, gb=gb, Sprev=Sprev):
                        ins = None
                        for hh in range(4):
                            for dvc in range(2):
                                r = ((hh % 2) * 2 + dvc) * 128
                                o = psum[pso[hh // 2]][:, r:r + 128]
                                ins = e.matmul(o, vt[:, b, hh * 256 + dvc * 128:hh * 256 + dvc * 128 + 128], PTt[:, hh * 128:(hh + 1) * 128], start=True, stop=(gb == 0))
                                if gb > 0:
                                    ins = e.matmul(o, Sprev[:, hh, dvc * 128:(dvc + 1) * 128], qdT[:, hh, bc0:bc0 + 128], start=False, stop=True)
                        return ins
                    p.add("pe", fn, [("vt", b, 0), ("vt", b, 1), ("PT", b % 2), ("Sb", (gb - 1) % 2)] + kk("qdT", range(4), [b]), [("ps", pso[0]), ("ps", pso[1])], ov="2")
                    epi_squares(pso, 128, b)
                    if bi + 1 < len(pblks):
                        s_chain(pblks[bi + 1])
                        scores_mask(pblks[bi + 1])
                    bgstep()
                    epi_rest(pso, 128, bc0, b)
                    ps_reserved.difference_update(pso)
                    if bi + 1 < len(pblks):
                        s_copy(pblks[bi + 1])
                    if bi == 1:
                        scale_wpb()
                    bgstep()

            def sample_steps():
                psos = [newps(), newps()]
                ps_reserved.update(psos)

                alias_deps = set()
                for key in [("uT", c, b_) for c in range(4) for b_ in range(5)] + [("vn", b_) for b_ in range(5)]:
                    if key in p.lastw:
                        alias_deps.add(p.lastw[key])
                    alias_deps.update(p.readers.get(key, ()))
                alias_deps = p._prune(alias_deps)

                def load_s0(m_):
                    sj = m_ % 4
                    p.add("sp", lambda e: e.dma_start(out=s0b[sj], in_=st_d[m_].rearrange("h k v -> k h v")),
                          writes=[("s0", sj)], dma_sem=f"s0l{sj}", ov=s0reg[sj],
                          extra_deps=(alias_deps if m_ in (2, 3) else ()))

                def make_kmask(m_):
                    kmm = kmask[m_ % 2]
                    op("dve", lambda e: e.tensor_scalar(kmm[:, :], ks_s[:, :], ident[0:16, m_:m_ + 1], None, ALU.mult),
                       ["ks_s", "ident"], [("km", m_ % 2)])

                def stage_a(n_):
                    si = n_ % 2
                    sq_ = n_ % 4
                    s0 = s0b[sq_]
                    if n_ == 0:
                        load_s0(0)
                        load_s0(1)
                        load_s0(2)
                    if n_ + 3 < NS:
                        load_s0(n_ + 3)
                    km = kmask[si]
                    if n_ == 0:
                        make_kmask(0)
                    psu_ = [newps(), newps()]

                    def fn(e):
                        ins = None
                        for hh in range(4):
                            ins = e.matmul(psum[psu_[hh // 2]][:, (hh % 2) * 256:(hh % 2) * 256 + 256], km[:, hh * 128:(hh + 1) * 128], vs_s[:, hh * 256:(hh + 1) * 256], start=True, stop=True)
                        return ins
                    p.add("pe", fn, [("km", si), ("vs_s", 0), ("vs_s", 1)], [("ps", psu_[0]), ("ps", psu_[1])])
                    for hh in range(4):
                        src = psum[psu_[hh // 2]][:, (hh % 2) * 256:(hh % 2) * 256 + 256]
                        op("dve", (lambda hh, src: lambda e: e.scalar_tensor_tensor(s0[:, hh, :], s0[:, hh, :], aTs[:, hh * 16 + n_:hh * 16 + n_ + 1], src, ALU.mult, ALU.add))(hh, src),
                           [("ps", psu_[hh // 2]), "aTs", ("s0", sq_)], [("s0", sq_)], ov=s0reg[sq_])
                    if n_ + 1 < NS:
                        make_kmask(n_ + 1)
                    p.add("sp", lambda e: e.dma_start(out=ss_d[n_].rearrange("h k v -> k h v"), in_=s0),
                          reads=[("s0", sq_)], dma_sem=f"s0s{sq_}", ov=s0reg[sq_])
                    sn = snb[si]
                    op("act", lambda e: e.activation(sn[:].rearrange("p h v -> p (h v)"), s0.rearrange("p h v -> p (h v)"), AF.Copy),
                       [("s0", sq_)], [("snb", si)], ov=s0reg[sq_])

                def stage_b(n_):
                    si = n_ % 2
                    sn = snb[si]

                    def fn(e):
                        ins = None
                        for hh in range(4):
                            for dvc in range(2):
                                r = ((hh % 2) * 2 + dvc) * NS + n_
                                ins = e.matmul(psum[psos[hh // 2]][:, r:r + 1], sn[:, hh, dvc * 128:(dvc + 1) * 128], qdT[:, hh, PT + n_:PT + n_ + 1], start=True, stop=True)
                        return ins
                    p.add("pe", fn, [("snb", si)] + kk("qdT", range(4), [4]), [("ps", psos[0]), ("ps", psos[1])], ov="2")

                for n_ in range(NS + 1):
                    if n_ < NS:
                        stage_a(n_)
                    if n_ >= 1:
                        stage_b(n_ - 1)
                    yield
                o_epilogue(psos, NS, PT, 4)
                ps_reserved.difference_update(psos)
                yield

            bg = sample_steps() if has_s else None

            def bgstep():
                if bg is not None:
                    next(bg, None)

            gbb = [load_piece(wview(win_d, KC, 5136 + i * 512, 512), KC, 512) for i in range(2)]
            wpb = [load_piece(wview(wpb_d, KC, i * 512, 512), KC, 512) for i in range(2)]

            def scale_wpb():
                for (wv, wk) in wpb:
                    for dvc in range(2):
                        op("dve", (lambda wv, dvc: lambda e: e.tensor_scalar(wv[:, dvc:KC:2, :], wv[:, dvc:KC:2, :], cols[:, C_GGLA + dvc:C_GGLA + dvc + 1], None, ALU.mult))(wv, dvc),
                           [wk, "c_cols"], [wk])
            if not has_s:
                p.fence("1")
                on_m11(range(2))
            prompt_recurrence(bgstep)

            def drain_bg():
                if bg is not None:
                    for _ in bg:
                        pass

            def m11_mid():
                drain_bg()
                if has_s:
                    p.fence("1")
                    p.fence("2")
                else:
                    p.fence("2")
                    on_m11(range(2, 6))
            if not has_s:
                m11_mid()
            def fin_b(c, seg, sg, ti, ps2, si):
                c0, n, blks = seg
                op("dve", (lambda: lambda e: e.tensor_tensor(sg, sg, psum[ps2][:, 0:n], ALU.mult))(),
                   [("tmp", ti), ("ps", ps2)], [("tmp", ti)])
                op("dve", (lambda: lambda e: e.tensor_tensor(ma[:, c, c0:c0 + n], sg, ma[:, c, c0:c0 + n], ALU.add))(),
                   [("tmp", ti)] + kk("ma", [c], blks), kk("ma", [c], blks), ov="A")
            if has_s:
                gated_proj(xT, xb, segs, gbb, wpb, KC, sgb, "sgb", fin_b, segmajor=True, hook=bgstep, mid=m11_mid)
            else:
                gated_proj(xT, xb, segs, gbb, wpb, KC, sgb, "sgb", fin_b)

            wop = [load_piece(wview(wout_d, KC, i * 512, 512), KC, 512) for i in range(2)]
            on_m12()
            act_prefetch_ln()
            stats_begin(segs, delay=2)
            for c in range(KC):
                pc, pk = wop[c // 4]
                for si, seg in enumerate(segs):
                    c0, n, blks = seg
                    ps = newps()
                    proj_fm(pc, pk, (c % 4) * 128, ma, "ma", KC, seg, ps, ov="A")
                    stats_pump()
                    op("dve", (lambda c, c0, n, ps: lambda e: e.tensor_tensor(xT[:, c, c0:c0 + n], psum[ps][:, 0:n], xT[:, c, c0:c0 + n], ALU.add))(c, c0, n, ps),
                       [("ps", ps)] + kk(("xT", xb), [c], blks), kk(("xT", xb), [c], blks))
                    stats_done(xT, xb, c, si, seg)
            p.fence("A")

        finals = []
        pre = {}
        xv = xT_d.rearrange("(k p) n -> p k n", p=128)
        pv_ = pT_d.rearrange("(k p) n -> p k n", p=128)

        def load_x(t):
            xb = t % 2
            xT = xTb[xb]
            p.add("sp", lambda e: e.dma_start(out=xT[:, :, 0:PT], in_=xv[:, :, t * PT:(t + 1) * PT]),
                  writes=kk(("xT", xb), range(KC), [0, 1, 2, 3]), dma_sem=f"xl{xb}")
            if t == 0:
                p.add("sp", lambda e: e.dma_start(out=xT[:, :, PT:TT], in_=xv[:, :, 2048:NTOK]),
                      writes=kk(("xT", xb), range(KC), [4]), dma_sem="xls")

        def load_p(t):
            xb = t % 2
            pTb = pTbs[xb]
            pkey = ("pTb", xb)
            p.add("pool", lambda e: e.dma_start(out=pTb[:, :, 0:PT], in_=pv_[:, :, t * PT:(t + 1) * PT]),
                  writes=kk(pkey, range(2), [0, 1, 2, 3]), dma_sem=f"pl{xb}")
            if t == 0:
                p.add("pool", lambda e: e.dma_start(out=pTb[:, :, PT:TT], in_=pv_[:, :, 2048:NTOK]),
                      writes=kk(pkey, range(2), [4]), dma_sem="pls")

        for t in range(NT):
            ncol, segs, blks_all = tile_info(t)
            xb = t % 2
            xT = xTb[xb]
            pTb = pTbs[xb]
            pkey = ("pTb", xb)
            if t == 0:
                load_x(1)
                pre["f1"] = pre0
            stats_begin(segs)
            for si, seg in enumerate(segs):
                for c in range(KC):
                    stats_feed(xT, xb, c, si, seg)

            ffn(xT, xb, segs, w1i_d, C_GFFN1, pre["f1"], wo_late=w1o_d)
            if t == 0:
                load_p(0)
            if t + 1 < NT:
                load_p(t + 1)
            mixer(t, xT, xb, ncol, segs, blks_all,
                  on_m11=lambda which: load_wo(w2o_d, which),
                  on_m12=lambda: pre.__setitem__("f2", ffn_pieces(w2i_d, 0)))
            ffn(xT, xb, segs, w2i_d, C_GFFN2, pre["f2"], wo_late=(w2o_d if t == 0 else None))

            rmsnorm_to_xn(xT, xb, segs, C_GPLE)
            wpg = [load_piece(wview(wpg_d, KC, i * 512, 512), KC, 512) for i in range(2)]
            wpl = load_piece(wview(wple_d, 2, 0, 1024), 2, 1024)
            if t + 1 < NT:
                pre["f1"] = ffn_pieces(w1i_d, 0)
            def fin_p(c, seg, sg, ti, ps2, si, xT=xT, xb=xb):
                c0, n, blks = seg
                op("dve", (lambda: lambda e: e.tensor_tensor(sg, sg, psum[ps2][:, 0:n], ALU.mult))(),
                   [("tmp", ti), ("ps", ps2)], [("tmp", ti)])
                op("dve", (lambda: lambda e: e.tensor_tensor(xT[:, c, c0:c0 + n], sg, xT[:, c, c0:c0 + n], ALU.add))(),
                   [("tmp", ti)] + kk(("xT", xb), [c], blks), kk(("xT", xb), [c], blks))
                stats_done(xT, xb, c, si, seg)
            stats_begin(segs, delay=2)
            gated_proj(xT, xb, segs, wpg, [wpl], 2, pTb, pkey, fin_p, last_prefetch=True)

            stats_finish(segs)
            for (c0, n, blks) in segs:
                for kc in range(KC):
                    op("dve", (lambda kc, c0, n, xT=xT: lambda e: e.scalar_tensor_tensor(
                        xT[:, kc, c0:c0 + n], xT[:, kc, c0:c0 + n], cols[:, C_GFIN + kc:C_GFIN + kc + 1], rinv[:, c0:c0 + n],
                        ALU.mult, ALU.mult))(kc, c0, n),
                       kk(("xT", xb), [kc], blks) + kk("rinv", [0], blks) + ["c_cols"], kk(("xT", xb), [kc], blks))
            yv = yT_d.rearrange("(k p) n -> p k n", p=128)
            finals.append(p.add("sp", (lambda t, xT: lambda e: e.dma_start(out=yv[:, :, t * PT:(t + 1) * PT], in_=xT[:, :, 0:PT]))(t, xT),
                                reads=kk(("xT", xb), range(KC), [0, 1, 2, 3]), dma_sem=f"ys{xb}"))
            if t == 0:
                finals.append(p.add("sp", (lambda xT: lambda e: e.dma_start(out=yv[:, :, 2048:NTOK], in_=xT[:, :, PT:TT]))(xT),
                                    reads=kk(("xT", xb), range(KC), [4]), dma_sem="yss"))
            if t + 2 < NT:
                load_x(t + 2)

        outs = [i for i, o in enumerate(p.ops) if o["dma_sem"] is not None and
                (o["dma_sem"].startswith("ys") or o["dma_sem"] in ("cv", "spo") or o["dma_sem"].startswith("s0s"))]
        p.add("sp", lambda e: e.nop(), extra_deps=outs)
        p.emit(st)
    return nc


_CACHE = {}


def _prep_inputs(inp):
    f = np.float32
    xp = np.asarray(inp["x_prompt"], f)
    xs = np.asarray(inp["x_sample"], f)
    pp = np.asarray(inp["p_prompt"], f)[0]
    ps_ = np.asarray(inp["p_sample"], f)[0]
    stg = np.asarray(inp["state_gla"], f)[0]

    def colpack(g):
        return np.ascontiguousarray(np.asarray(g, f).reshape(-1, 128).T)

    cols = np.concatenate([colpack(inp["g_ffn1"][0]), colpack(inp["g_mix"][0]), colpack(inp["g_ffn2"][0]),
                           colpack(inp["g_ple"][0]), colpack(inp["g_final"]), colpack(inp["g_gla_out"][0])], axis=1)
    shared = {
        "cols": np.ascontiguousarray(cols, f),
        "w_ffn1_in": np.ascontiguousarray(inp["w_ffn1_in"][0], f),
        "w_ffn1_out": np.ascontiguousarray(inp["w_ffn1_out"][0], f),
        "w_in": np.ascontiguousarray(inp["w_in"][0], f),
        "ln_v_g": np.ascontiguousarray(inp["ln_v_g"][0], f),
        "ln_v_b": np.ascontiguousarray(inp["ln_v_b"][0], f),
        "wsT": np.ascontiguousarray(np.transpose(np.asarray(inp["w_spatial"][0], f), (0, 2, 1))),
        "w_spatial": np.ascontiguousarray(inp["w_spatial"][0], f),
        "b_spatial": np.ascontiguousarray(inp["b_spatial"][0], f),
        "w_gate_up": np.ascontiguousarray(inp["w_gate_up"][0], f),
        "b_gate": np.ascontiguousarray(inp["b_gate"][0], f),
        "w_proj_a": np.ascontiguousarray(inp["w_proj_a"][0], f),
        "w_proj_b": np.ascontiguousarray(inp["w_proj_b"][0], f),
        "w_out": np.ascontiguousarray(inp["w_out"][0], f),
        "w_ffn2_in": np.ascontiguousarray(inp["w_ffn2_in"][0], f),
        "w_ffn2_out": np.ascontiguousarray(inp["w_ffn2_out"][0], f),
        "w_ple_gate": np.ascontiguousarray(inp["w_ple_gate"][0], f),
        "w_ple": np.ascontiguousarray(inp["w_ple"][0], f),
    }
    maps = []
    for i in range(NCORES):
        sl = slice(NS * i, NS * (i + 1))
        xT = np.concatenate([xp[i].T, xs[sl, 0, :].T], axis=1)
        pT = np.concatenate([pp[i].T, ps_[sl, 0, :].T], axis=1)
        m = dict(shared)
        m["xT"] = np.ascontiguousarray(xT, f)
        m["pT"] = np.ascontiguousarray(pT, f)
        m["st"] = np.ascontiguousarray(stg[sl], f)
        maps.append(m)
    return maps


def kernel(**inputs):
    if "nc" not in _CACHE:
        _CACHE["nc"] = build_program()
    nc = _CACHE["nc"]
    maps = _prep_inputs(inputs)
    res = run_bass_kernel_spmd(nc, maps, core_ids=list(range(NCORES)))
    r = res.results
    y_prompt = np.stack([r[i]["yT"][:, :2048].T for i in range(NCORES)]).astype(np.float32)
    y_sample = np.concatenate([r[i]["yT"][:, 2048:].T for i in range(NCORES)])[:, None, :].astype(np.float32)
    sp = np.stack([r[i]["sp_out"] for i in range(NCORES)])[None].astype(np.float32)
    ss = np.concatenate([r[i]["ss_out"] for i in range(NCORES)])[None].astype(np.float32)
    cv = np.concatenate([r[i]["cv_out"] for i in range(NCORES)])[None, :, None, :].astype(np.float32)
    return (np.ascontiguousarray(y_prompt), np.ascontiguousarray(y_sample), np.ascontiguousarray(sp),
            np.ascontiguousarray(ss), np.ascontiguousarray(cv))
```

```python
import math
from contextlib import ExitStack

import numpy as np
import concourse.bass as bass
import concourse.mybir as mybir
from concourse.bass_utils import run_bass_kernel_spmd

F32 = mybir.dt.float32
BF16 = mybir.dt.bfloat16
AF = mybir.ActivationFunctionType
ALU = mybir.AluOpType

ENGS = ("pe", "act", "dve", "pool", "sp")
EPS = 1e-6
NCORES = 8
D = 1024
KC = 8
DFF = 2816
JH = 22
PT = 512
NT = 4
NS = 16
TT = PT + NS
NTOK = 2048 + NS
SLOT = 4096
NSLOT = 5


class _FirstMatmul:
    def __init__(self, eng):
        self._e = eng
        self.first = None

    def matmul(self, *a, **k):
        ins = self._e.matmul(*a, **k)
        if self.first is None:
            self.first = ins
        return ins

    def __getattr__(self, name):
        return getattr(self._e, name)


class Prog:
    def __init__(self, nc):
        self.nc = nc
        self.ops = []
        self.lastw = {}
        self.readers = {}
        self.fence_deps = {}

    def fence(self, region):
        key = "OVuse" + region
        deps = set(self.readers.get(key, ()))
        self.fence_deps[region] = self._prune(deps)
        self.readers[key] = []

    def _prune(self, deps):
        best = {}
        keep = set()
        for d in deps:
            o = self.ops[d]
            if o["dma_sem"] is not None:
                keep.add(d)
            else:
                e = o["eng"]
                if e not in best or best[e] < d:
                    best[e] = d
        keep.update(best.values())
        return keep

    def add(self, eng, fn, reads=(), writes=(), dma_sem=None, extra_deps=(), ov=False):
        idx = len(self.ops)
        deps = set(extra_deps)
        reads = list(reads)
        if ov:
            for region in ov:
                reads.append("OVuse" + region)
                deps.update(self.fence_deps.get(region, ()))
        psdeps = set()
        for k in reads:
            w = self.lastw.get(k)
            if w is not None:
                deps.add(w)
        for k in writes:
            isps = isinstance(k, tuple) and k[0] == "ps"
            tgt = psdeps if isps else deps
            w = self.lastw.get(k)
            if w is not None:
                tgt.add(w)
            tgt.update(self.readers.get(k, ()))
        psdeps -= deps
        deps |= psdeps
        for k in reads:
            self.readers.setdefault(k, []).append(idx)
        for k in writes:
            self.lastw[k] = idx
            self.readers[k] = []
        keep = self._prune(deps)
        if eng == "pe":
            keep = {d for d in keep if not (self.ops[d]["eng"] == "pe" and self.ops[d]["dma_sem"] is None)}
        self.ops.append(dict(eng=eng, fn=fn, deps=keep, dma_sem=dma_sem, psdeps=(psdeps & keep)))
        return idx

    def emit(self, stack):
        nc = self.nc
        ops = self.ops
        for o in ops:
            o["target"] = False
        for o in ops:
            for d in o["deps"]:
                ops[d]["target"] = True
        cnt = {}
        semnames = set()
        for o in ops:
            if o["dma_sem"] is not None:
                s = "d_" + o["dma_sem"]
                cnt[s] = cnt.get(s, 0) + 16
                o["semval"] = (s, cnt[s])
                semnames.add(s)
            elif o["target"]:
                s = "e_" + o["eng"]
                cnt[s] = cnt.get(s, 0) + 1
                o["semval"] = (s, cnt[s])
                semnames.add(s)
        sems = {s: stack.enter_context(nc.semaphore(s)) for s in sorted(semnames)}
        self.nsems = len(sems)
        block = stack.enter_context(nc.Block())
        per_eng = {e: [o for o in ops if o["eng"] == e] for e in ENGS}

        def make(ename):
            def body(eng):
                seen = {}
                for o in per_eng[ename]:
                    dl = sorted(((ops[d]["semval"], d) for d in o["deps"]), key=lambda t: -t[0][1])
                    need = []
                    for (s, v), d in dl:
                        if seen.get(s, 0) >= v:
                            continue
                        need.append((s, v, d))
                        seen[s] = v
                    attach = max(need, key=lambda t3: t3[2]) if need else None
                    for t3 in need:
                        if t3 is not attach:
                            eng.wait_ge(sems[t3[0]], t3[1])
                    if attach is not None and ename == "pe":
                        rec = _FirstMatmul(eng)
                        ins = o["fn"](rec)
                        rec.first._wait_ge(sems[attach[0]], attach[1])
                    else:
                        ins = o["fn"](eng)
                        if attach is not None:
                            ins._wait_ge(sems[attach[0]], attach[1])
                    if o["dma_sem"] is not None:
                        ins.then_inc(sems[o["semval"][0]], 16)
                    elif o["target"]:
                        ins.then_inc(sems[o["semval"][0]], 1)
            return body

        block.tensor(make("pe"))
        block.scalar(make("act"))
        block.vector(make("dve"))
        block.gpsimd(make("pool"))
        block.sync(make("sp"))


C_GFFN1, C_GMIX, C_GFFN2, C_GPLE, C_GFIN, C_GGLA, NCOLS = 0, 8, 16, 24, 32, 40, 42


def build_program(debug_taps=()):
    nc = bass.Bass("TRN2", target_bir_lowering=False)

    def din(name, shape):
        return nc.dram_tensor(name, list(shape), F32, kind="ExternalInput").ap()

    def dout(name, shape):
        return nc.dram_tensor(name, list(shape), F32, kind="ExternalOutput").ap()

    xT_d = din("xT", [D, NTOK])
    pT_d = din("pT", [256, NTOK])
    st_d = din("st", [NS, 4, 128, 256])
    cols_d = din("cols", [128, NCOLS])
    w1i_d = din("w_ffn1_in", [D, 2 * DFF])
    w1o_d = din("w_ffn1_out", [DFF, D])
    win_d = din("w_in", [D, 6160])
    lng_d = din("ln_v_g", [512])
    lnb_d = din("ln_v_b", [512])
    wsT_d = din("wsT", [4, 128, 128])
    wsp_d = din("w_spatial", [4, 128, 128])
    bsp_d = din("b_spatial", [4, 128])
    wgu_d = din("w_gate_up", [16, 512])
    bg_d = din("b_gate", [512])
    wpa_d = din("w_proj_a", [512, D])
    wpb_d = din("w_proj_b", [D, D])
    wout_d = din("w_out", [D, D])
    w2i_d = din("w_ffn2_in", [D, 2 * DFF])
    w2o_d = din("w_ffn2_out", [DFF, D])
    wpg_d = din("w_ple_gate", [D, D])
    wple_d = din("w_ple", [256, D])

    yT_d = dout("yT", [D, NTOK])
    sp_d = dout("sp_out", [4, 128, 256])
    ss_d = dout("ss_out", [NS, 4, 128, 256])
    cv_d = dout("cv_out", [NS, 512])
    dbg_d = {n: dout("dbg_" + n, shp) for n, shp in debug_taps}

    st = ExitStack()
    with st:
        def sb(name, shape, dt):
            return st.enter_context(nc.sbuf_tensor("s_" + name, list(shape), dt))

        p = Prog(nc)

        xTb = [sb("xT0", [128, KC, TT], F32), sb("xT1", [128, KC, TT], F32)]
        xn = sb("xn", [128, KC, TT], BF16)
        rinv = sb("rinv", [128, TT], F32)
        ring = [sb(f"ring{i}", [128, SLOT], BF16) for i in range(NSLOT)]
        tmps = [sb(f"tmp{i}", [128, 512], F32) for i in range(4)]
        OVA, OVB = 6272, 11656
        ova = sb("ova", [128, OVA], F32)
        ovb_ = sb("ovb", [128, OVB], F32)
        psum = [st.enter_context(nc.psum_tensor(f"ps{i}", [128, 512], F32)) for i in range(8)]

        h_v = ova[:, 0:5808].bitcast(BF16).rearrange("p (j t) -> p j t", j=JH)
        wo_v = ovb_[:, 0:11264].bitcast(BF16).rearrange("p (j n) -> p j n", j=JH)
        ma = ova[:, 0:2112].bitcast(BF16).rearrange("p (c t) -> p c t", c=8)
        sgb = ova[:, 2112:4224].bitcast(BF16).rearrange("p (c t) -> p c t", c=8)
        s0b = [ova[:, 4224:5248].rearrange("p (h v) -> p h v", h=4),
               ova[:, 5248:6272].rearrange("p (h v) -> p h v", h=4),
               ovb_[:, 0:1024].rearrange("p (h v) -> p h v", h=4),
               ovb_[:, 1024:2048].rearrange("p (h v) -> p h v", h=4)]
        s0reg = ["A", "A", "1", "1"]
        _o = [0]

        def ovb(nwords, shape_str=None, **kw):
            a = _o[0]
            _o[0] += nwords
            assert _o[0] <= OVB, _o[0]
            v = ovb_[:, a:a + nwords].bitcast(BF16)
            if shape_str:
                v = v.rearrange(shape_str, **kw)
            return v

        uT = ovb(1056, "p (c t) -> p c t", c=4)
        vn = ovb(1280, "p (b f) -> p b f", b=5)
        ETv = ovb(1056, "p (c t) -> p c t", c=4)
        EIv = ovb(1056, "p (c t) -> p c t", c=4)
        Krev = ovb(1024, "p (b f) -> p b f", b=4)
        qdT = ovb(1056, "p (c t) -> p c t", c=4)
        kdT = ovb(1056, "p (c t) -> p c t", c=4)
        kst = ovb(1024, "p (b f) -> p b f", b=4)
        vt = ovb(2048, "p (b f) -> p b f", b=4)

        cvf = sb("cvf", [16, 512], F32)
        lrT = sb("lrT", [17, TT], BF16)
        lp = [sb("lp0", [128, 512], F32), sb("lp1", [128, 512], F32)]
        alast = sb("alast", [128, 4, 4], F32)
        ks_s = sb("ks_s", [16, 512], BF16)
        vs_s = sb("vs_s", [16, 1024], BF16)
        PTb = [sb("PT0", [128, 512], BF16), sb("PT1", [128, 512], BF16)]
        sqb = [sb("sq0", [128, 1024], BF16), sb("sq1", [128, 1024], BF16)]
        Sf = sb("Sf", [128, 4, 256], F32)
        Sbs = [sb("Sb0", [128, 4, 256], BF16), sb("Sb1", [128, 4, 256], BF16)]
        snb = [sb("snb0", [128, 4, 256], BF16), sb("snb1", [128, 4, 256], BF16)]
        kmask = [sb("km0", [16, 512], BF16), sb("km1", [16, 512], BF16)]
        aTs = sb("aTs", [128, 64], F32)
        lpTs = sb("lpTs", [128, 64], F32)
        stat6 = sb("stat6", [128, 6], F32)
        rstd4 = sb("rstd4", [128, 5], F32)
        mv4 = sb("mv4", [128, 5, 2], F32)
        pTbs = [sb("pTb0", [128, 2, TT], BF16), sb("pTb1", [128, 2, TT], BF16)]
        cols = sb("cols", [128, NCOLS], F32)
        maskT = sb("maskT", [128, 4, 128], BF16)
        lng = sb("lng", [128, 512], F32)
        lnb = sb("lnb", [128, 512], F32)
        Uneg = sb("Uneg", [128, 128], F32)
        Lneg = sb("Lneg", [128, 128], F32)
        ones_b = sb("ones_b", [128, 128], BF16)
        onesf = sb("onesf", [128, 128], F32)
        ident = sb("ident", [128, 128], F32)
        WsT = sb("WsT", [128, 4, 128], BF16)
        brow = sb("brow", [1, 4, 128], BF16)
        bcol = sb("bcol", [128, 4], F32)
        w00 = sb("w00", [128, 4], F32)
        Dg = sb("Dg", [16, 4, 16], BF16)
        wgu = sb("wgu", [17, 512], BF16)
        wlr = sb("wlr", [128, KC, 16], BF16)
        dummy = sb("dummy", [1, 2], F32)

        psc = [0]
        ps_reserved = set()

        def newps():
            assert len(ps_reserved) < 8, "all PSUM banks reserved"
            while True:
                i = psc[0] % 8
                psc[0] += 1
                if i not in ps_reserved:
                    return i

        tmc = [0]

        def newtmp():
            i = tmc[0] % 4
            tmc[0] += 1
            return i

        slc = [0]

        def kk(name, idxs, blks):
            return [(name, i, b) for i in idxs for b in blks]

        def mm(out, pairs, reads, writes, ov=False, pair_reads=None):
            if pair_reads is not None:
                n = len(pairs)
                last = None
                for i, (l, r) in enumerate(pairs):
                    def fn1(e, i=i, l=l, r=r):
                        return e.matmul(out, l, r, start=(i == 0), stop=(i == n - 1))
                    last = p.add("pe", fn1, list(reads) + list(pair_reads[i]), writes, ov=ov)
                return last

            def fn(e):
                n = len(pairs)
                ins = None
                for i, (l, r) in enumerate(pairs):
                    ins = e.matmul(out, l, r, start=(i == 0), stop=(i == n - 1))
                return ins
            return p.add("pe", fn, reads, writes, ov=ov)

        def op(eng, f, reads, writes, ov=False):
            return p.add(eng, f, reads, writes, ov=ov)

        def load_piece(src3, kc, ncols):
            s = slc[0] % NSLOT
            slc[0] += 1
            view = ring[s][:, 0:kc * ncols].rearrange("p (k n) -> p k n", k=kc)
            p.add("pool", lambda e: e.dma_start(out=view, in_=src3), writes=[("slot", s)], dma_sem=f"slot{s}")
            return view, ("slot", s)

        def wview(w_d, kc, c0, ncols, r0=0):
            return w_d.rearrange("(k p) n -> p k n", p=128)[:, r0:r0 + kc, c0:c0 + ncols]

        xv = xT_d.rearrange("(k p) n -> p k n", p=128)
        p.add("sp", lambda e: e.dma_start(out=xTb[0][:, :, 0:PT], in_=xv[:, :, 0:PT]),
              writes=kk(("xT", 0), range(KC), [0, 1, 2, 3]), dma_sem="xl0")
        p.add("sp", lambda e: e.dma_start(out=xTb[0][:, :, PT:TT], in_=xv[:, :, 2048:NTOK]),
              writes=kk(("xT", 0), range(KC), [4]), dma_sem="xls")
        pre0 = (load_piece(wview(w1i_d, KC, 0, 512), KC, 512), load_piece(wview(w1i_d, KC, DFF, 512), KC, 512))

        cdma = []

        def cload(eng, out, in_, slow=False):
            cdma.append(p.add(eng, lambda e: e.dma_start(out=out, in_=in_, allow_slow_non_contiguous=slow), dma_sem="c_" + eng))

        p.add("sp", lambda e: e.dma_start(out=cols[:], in_=cols_d), writes=["c_cols"], dma_sem="ccols")
        op("dve", lambda e: e.memset(onesf[:], 1.0), [], ["onesf"])
        op("dve", lambda e: e.memset(ones_b[:], 1.0), [], ["ones_b"])
        op("dve", lambda e: e.memset(lrT[:], 1.0), [], ["lrT_init"])
        op("dve", lambda e: e.memset(mv4[:], 1.0), [], ["mv4_init"])

        def late_setup():
            cload("pool", lng[:], lng_d.partition_broadcast(128))
            cload("pool", lnb[:], lnb_d.partition_broadcast(128))
            wsf = lp[0][:, :].rearrange("p (g t) -> p g t", g=4)
            cload("pool", wsf, wsT_d.rearrange("g s t -> s g t"))
            cload("pool", bcol[:], bsp_d[:, 0].partition_broadcast(128), slow=True)
            cload("pool", w00[:], wsp_d[:, 0, 0].partition_broadcast(128), slow=True)
            cload("pool", brow[:], bsp_d.rearrange("(o g) t -> o g t", o=1))
            cload("pool", wgu[0:16, :], wgu_d)
            cload("pool", wgu[16:17, :], bg_d.rearrange("(o n) -> o n", o=1))
            cload("pool", wlr[:], wview(win_d, KC, 4096, 16))
            CK = ["c_ln", "c_wsf", "c_bcol", "c_w00", "c_brow", "c_wgu", "c_wlr"]
            p.add("sp", lambda e: e.nop(), writes=CK, extra_deps=cdma)

            op("pool", lambda e: e.memset(Uneg[:], -1.0 / 16.0), [], ["Uneg"])
            op("pool", lambda e: e.memset(Lneg[:], -1.0 / 16.0), [], ["Lneg"])
            op("pool", lambda e: e.affine_select(Uneg[:], Uneg[:], [[1, 128]], ALU.is_ge, 0.0, base=0, channel_multiplier=-1), ["Uneg"], ["Uneg"])
            op("pool", lambda e: e.affine_select(Lneg[:], Lneg[:], [[-1, 128]], ALU.is_gt, 0.0, base=0, channel_multiplier=1), ["Lneg"], ["Lneg"])
            op("pool", lambda e: e.affine_select(ident[:], onesf[:], [[1, 128]], ALU.is_equal, 0.0, base=0, channel_multiplier=-1), ["onesf"], ["ident"])
            op("pool", lambda e: e.affine_select(maskT[:], onesf[:].unsqueeze(1).to_broadcast([128, 4, 128]), [[0, 4], [1, 128]], ALU.is_ge, 0.0, base=0, channel_multiplier=-1), ["onesf"], ["maskT"])
            op("pool", lambda e: e.affine_select(WsT[:], wsf, [[0, 4], [1, 128]], ALU.is_ge, 0.0, base=0, channel_multiplier=-1), ["c_wsf", ("lp", 0)], ["WsT"])


        def tile_info(t):
            if t == 0:
                return TT, [(0, 256, [0, 1]), (256, 272, [2, 3, 4])], [0, 1, 2, 3, 4]
            return PT, [(0, 512, [0, 1, 2, 3])], [0, 1, 2, 3]

        def blkcols(b):
            return (b * 128, 128) if b < 4 else (PT, NS)

        stat = {}

        def stats_begin(segs, delay=1):
            stat["ps"] = [newps() for _ in segs]
            ps_reserved.update(stat["ps"])
            stat["cnt"] = [0] * len(segs)
            stat["pending"] = []
            stat["delay"] = delay

        def stats_feed(xT, xb, c, si, seg):
            c0, n, blks = seg
            ps = stat["ps"][si]
            ti = newtmp()
            sq = tmps[ti][:, 0:256].bitcast(BF16)[:, 0:n]
            op("act", lambda e: e.activation(sq, xT[:, c, c0:c0 + n], AF.Square), kk(("xT", xb), [c], blks), [("tmp", ti)])
            k = stat["cnt"][si]
            stat["cnt"][si] += 1
            p.add("pe", lambda e: e.matmul(psum[ps][:, 0:n], ones_b[:], sq, start=(k == 0), stop=(k == KC - 1)),
                  [("tmp", ti), "ones_b"], [("ps", ps)])

        def stats_done(xT, xb, c, si, seg):
            stat["pending"].append((xT, xb, c, si, seg))

        def stats_pump():
            while stat.get("pending"):
                stats_feed(*stat["pending"].pop(0))

        def stats_flush():
            while stat["pending"]:
                stats_feed(*stat["pending"].pop(0))

        def act_prefetch_ln():
            op("act", lambda e: e.activation(dummy[0:1, 0:1], onesf[0:1, 0:1], AF.Ln), ["onesf"], ["dummy"])

        def stats_finish(segs):
            stats_flush()
            for si, (c0, n, blks) in enumerate(segs):
                assert stat["cnt"][si] == KC
                ps = stat["ps"][si]
                ti = newtmp()
                lt = tmps[ti][:, 0:n]
                op("act", (lambda lt, ps, n: lambda e: e.activation(lt, psum[ps][:, 0:n], AF.Ln, bias=EPS, scale=1.0 / D))(lt, ps, n),
                   [("ps", ps)], [("tmp", ti)])
                op("act", (lambda lt, c0, n: lambda e: e.activation(rinv[:, c0:c0 + n], lt, AF.Exp, scale=-0.5))(lt, c0, n),
                   [("tmp", ti)], kk("rinv", [0], blks))
            ps_reserved.difference_update(stat["ps"])

        def rmsnorm_to_xn(xT, xb, segs, gc):
            stats_finish(segs)
            for (c0, n, blks) in segs:
                for kc in range(KC):
                    op("dve", (lambda kc, c0, n: lambda e: e.scalar_tensor_tensor(
                        xn[:, kc, c0:c0 + n], xT[:, kc, c0:c0 + n], cols[:, gc + kc:gc + kc + 1], rinv[:, c0:c0 + n],
                        ALU.mult, ALU.mult))(kc, c0, n),
                       kk(("xT", xb), [kc], blks) + kk("rinv", [0], blks) + ["c_cols"], kk("xn", [kc], blks))

        def proj_fm(piece, pkey, off, rhs_buf, rkey, nkc, seg, ps, ov=False):
            c0, n, blks = seg
            pairs = [(piece[:, kc, off:off + 128], rhs_buf[:, kc, c0:c0 + n]) for kc in range(nkc)]
            mm(psum[ps][:, 0:n], pairs, [pkey], [("ps", ps)], ov=ov, pair_reads=[kk(rkey, [kc], blks) for kc in range(nkc)])

        def load_wo(w_out_d, which=range(6)):
            wo_view = w_out_d.rearrange("(j p) n -> p j n", p=128)
            for i in which:
                j0 = 4 * i
                nj = min(4, JH - j0)
                p.add("pool", (lambda j0, nj: lambda e: e.dma_start(out=wo_v[:, j0:j0 + nj, :], in_=wo_view[:, j0:j0 + nj, :]))(j0, nj),
                      writes=[("wo", i)], dma_sem=f"wo{i}", ov=("1" if i < 2 else ("12" if i == 2 else "2")))

        def ffn_pieces(w_in_d, G):
            j0 = 4 * G
            nj = min(4, JH - j0)
            return (load_piece(wview(w_in_d, KC, j0 * 128, nj * 128), KC, nj * 128),
                    load_piece(wview(w_in_d, KC, DFF + j0 * 128, nj * 128), KC, nj * 128))

        def proj_fm_multi(items, rhs_buf, rkey, nkc, seg):
            c0, n, blks = seg
            last = len(items) - 1
            for kc in range(nkc):
                for (piece, pkey, off, ps) in items:
                    def fn1(e, piece=piece, off=off, ps=ps, kc=kc):
                        return e.matmul(psum[ps][:, 0:n], piece[:, kc, off:off + 128], rhs_buf[:, kc, c0:c0 + n], start=(kc == 0), stop=(kc == nkc - 1))
                    p.add("pe", fn1, [pkey] + kk(rkey, [kc], blks), [("ps", ps)])

        def ffn(xT, xb, segs, w_in_d, gc, pre, wo_late=None):
            rmsnorm_to_xn(xT, xb, segs, gc)
            for G in range(6):
                j0 = 4 * G
                nj = min(4, JH - j0)
                (pg, kg), (pu, ku) = pre if G == 0 else ffn_pieces(w_in_d, G)
                if wo_late is not None and G >= 1:
                    load_wo(wo_late, [G - 1] if G < 5 else [4, 5])
                first = {}
                if G == 0:
                    for seg in segs:
                        items = []
                        for jj in range(2):
                            psg, psu = newps(), newps()
                            ps_reserved.update((psg, psu))
                            first[(jj, seg[0])] = (psg, psu)
                            items += [(pg, kg, jj * 128, psg), (pu, ku, jj * 128, psu)]
                        proj_fm_multi(items, xn, "xn", KC, seg)
                for jj in range(nj):
                    j = j0 + jj
                    for seg in segs:
                        c0, n, blks = seg
                        if (jj, c0) in first:
                            psg, psu = first[(jj, c0)]
                            ps_reserved.difference_update((psg, psu))
                        else:
                            psg = newps()
                            proj_fm(pg, kg, jj * 128, xn, "xn", KC, seg, psg)
                            psu = newps()
                            proj_fm(pu, ku, jj * 128, xn, "xn", KC, seg, psu)
                        ti = newtmp()
                        sg = tmps[ti][:, 0:n]
                        op("act", (lambda sg, psg, n: lambda e: e.activation(sg, psum[psg][:, 0:n], AF.Silu))(sg, psg, n),
                           [("ps", psg)], [("tmp", ti)])
                        op("dve", (lambda sg, psu, j, c0, n: lambda e: e.tensor_tensor(h_v[:, j, c0:c0 + n], sg, psum[psu][:, 0:n], ALU.mult))(sg, psu, j, c0, n),
                           [("tmp", ti), ("ps", psu)], kk("h", [j], blks), ov="A")
            act_prefetch_ln()
            stats_begin(segs)
            for c in range(KC):
                for si, seg in enumerate(segs):
                    c0, n, blks = seg
                    ps = newps()
                    pairs = [(wo_v[:, j, c * 128:(c + 1) * 128], h_v[:, j, c0:c0 + n]) for j in range(JH)]
                    mm(psum[ps][:, 0:n], pairs, [("wo", i) for i in range(6)] + kk("h", range(JH), blks), [("ps", ps)], ov="A12")
                    stats_pump()
                    op("dve", (lambda c, c0, n, ps: lambda e: e.scalar_tensor_tensor(
                        xT[:, c, c0:c0 + n], psum[ps][:, 0:n], 0.5, xT[:, c, c0:c0 + n], ALU.mult, ALU.add))(c, c0, n, ps),
                       [("ps", ps)] + kk(("xT", xb), [c], blks), kk(("xT", xb), [c], blks))
                    stats_done(xT, xb, c, si, seg)

        def gated_proj(xT, xb, segs, gate_pieces, w2_pieces, nkc2, rhs2, rkey2, final, segmajor=False, hook=None, mid=None, last_prefetch=False):
            if segmajor:
                order = [(c, si) for si in range(len(segs)) for c in range(KC)]
            else:
                order = [(c, si) for c in range(KC) for si in range(len(segs))]
            for oi, (c, si) in enumerate(order):
                if segmajor and si == 1 and c == 0 and mid is not None:
                    mid()
                gp, gk = gate_pieces[c // 4]
                wp, wk = w2_pieces[(c // 4) if len(w2_pieces) > 1 else 0]
                off2 = (c % 4) * 128 if len(w2_pieces) > 1 else c * 128
                if True:
                    seg = segs[si]
                    c0, n, blks = seg
                    ps1 = newps()
                    proj_fm(gp, gk, (c % 4) * 128, xn, "xn", KC, seg, ps1)
                    ps2 = newps()
                    proj_fm(wp, wk, off2, rhs2, rkey2, nkc2, seg, ps2, ov={"uT": "1", "sgb": "A"}.get(rkey2, False) if isinstance(rkey2, str) else False)
                    stats_pump()
                    ti = newtmp()
                    sg = tmps[ti][:, 0:n]
                    op("act", (lambda sg, ps1, n: lambda e: e.activation(sg, psum[ps1][:, 0:n], AF.Sigmoid))(sg, ps1, n),
                       [("ps", ps1)], [("tmp", ti)])
                    if last_prefetch and oi == len(order) - 1:
                        act_prefetch_ln()
                    final(c, seg, sg, ti, ps2, si)
                    if segmajor and si == 0 and hook is not None:
                        hook()

        def mixer(t, xT, xb, ncol, segs, blks_all, on_m11, on_m12):
            pblks = [b for b in blks_all if b < 4]
            has_s = 4 in blks_all
            p.fence("A")
            p.fence("1")
            p.fence("2")
            if has_s:
                op("dve", lambda e: e.memset(ETv[:, :, PT:TT], 128.0 ** -0.5), [], kk("ET", range(4), [4]), ov="1")
                op("dve", lambda e: e.memset(EIv[:, :, PT:TT], 1.0), [], kk("EI", range(4), [4]), ov="1")
            rmsnorm_to_xn(xT, xb, segs, C_GMIX)

            pc, pk = load_piece(wview(win_d, KC, 0, 512), KC, 512)
            m1ps = {}
            for seg in segs:
                items = []
                for c in range(4):
                    m1ps[(c, seg[0])] = newps()
                    ps_reserved.add(m1ps[(c, seg[0])])
                    items.append((pc, pk, c * 128, m1ps[(c, seg[0])]))
                proj_fm_multi(items, xn, "xn", KC, seg)
            for c in range(4):
                for seg in segs:
                    c0, n, blks = seg
                    ps = m1ps[(c, c0)]
                    ps_reserved.discard(ps)
                    op("act", (lambda c, c0, n, ps: lambda e: e.activation(uT[:, c, c0:c0 + n], psum[ps][:, 0:n], AF.Gelu_apprx_tanh))(c, c0, n, ps),
                       [("ps", ps)], kk("uT", [c], blks), ov="1")

            pc, pk = load_piece(wview(win_d, KC, 512, 512), KC, 512)
            psl = {}
            for b in blks_all:
                bc0, m = blkcols(b)
                ps = newps()
                ps_reserved.add(ps)
                psl[b] = ps
                pairs = [(xn[:, kc, bc0:bc0 + m], pc[:, kc, :]) for kc in range(KC)]
                mm(psum[ps][0:m, :], pairs, [pk] + kk("xn", range(KC), [b]), [("ps", ps)])
                op("act", (lambda ps, m: lambda e: e.activation(psum[ps][0:m, :], psum[ps][0:m, :], AF.Gelu_apprx_tanh))(ps, m),
                   [("ps", ps)], [("ps", ps)])
                op("dve", (lambda ps, m: lambda e: e.bn_stats(stat6[0:m, :], psum[ps][0:m, :]))(ps, m), [("ps", ps)], ["stat6"])
                op("dve", (lambda b, m: lambda e: e.bn_aggr(mv4[0:m, b, :], stat6[0:m, :]))(b, m), ["stat6", "mv4_init"], [("mv4", b)])
            nb = len(blks_all)
            op("act", (lambda nb: lambda e: e.activation(rstd4[:, 0:nb], mv4[:, 0:nb, 1], AF.Ln, bias=EPS))(nb),
               [("mv4", b) for b in blks_all], ["rstd4"])
            op("act", (lambda nb: lambda e: e.activation(rstd4[:, 0:nb], rstd4[:, 0:nb], AF.Exp, scale=-0.5))(nb), ["rstd4"], ["rstd4"])
            for b in blks_all:
                bc0, m = blkcols(b)
                ps = psl[b]
                t2 = newtmp()
                ta = tmps[t2][0:m, :]
                op("dve", (lambda ps, ta, b, m: lambda e: e.scalar_tensor_tensor(ta, psum[ps][0:m, :], mv4[0:m, b, 0:1], lng[0:m, :], ALU.subtract, ALU.mult))(ps, ta, b, m),
                   [("ps", ps), ("mv4", b), "c_ln"], [("tmp", t2)])
                if b < 4:
                    op("dve", (lambda ta, b, m: lambda e: e.scalar_tensor_tensor(vn[0:m, b, :], ta, rstd4[0:m, b:b + 1], lnb[0:m, :], ALU.mult, ALU.add))(ta, b, m),
                       [("tmp", t2), "rstd4", "c_ln"], [("vn", b)], ov="1")
                else:
                    op("dve", (lambda ta, b, m: lambda e: e.scalar_tensor_tensor(cvf[0:m, :], ta, rstd4[0:m, b:b + 1], lnb[0:m, :], ALU.mult, ALU.add))(ta, b, m),
                       [("tmp", t2), "rstd4", "c_ln"], ["cvf"])
                    op("dve", lambda e: e.tensor_copy(vn[0:NS, 4, :], cvf[:]), ["cvf"], [("vn", 4)], ov="1")
                    p.add("sp", lambda e: e.dma_start(out=cv_d, in_=cvf[:]), reads=["cvf"], dma_sem="cv")
            ps_reserved.difference_update(psl.values())

            for seg in segs:
                c0, n, blks = seg
                ps = newps()
                pairs = [(wlr[:, kc, :], xn[:, kc, c0:c0 + n]) for kc in range(KC)]
                mm(psum[ps][0:16, 0:n], pairs, ["c_wlr"] + kk("xn", range(KC), blks), [("ps", ps)])
                op("act", (lambda ps, c0, n: lambda e: e.activation(lrT[0:16, c0:c0 + n], psum[ps][0:16, 0:n], AF.Copy))(ps, c0, n),
                   [("ps", ps), "lrT_init"], kk("lrT", [0], blks))
            pv2 = [load_piece(wview(win_d, KC, 2048 + i * 512, 512), KC, 512) for i in range(2)]

            def v_block(b):
                bc0, m = blkcols(b)
                for hv in range(2):
                    pc, pk = pv2[hv]
                    ps = newps()
                    pairs = [(xn[:, kc, bc0:bc0 + m], pc[:, kc, :]) for kc in range(KC)]
                    mm(psum[ps][0:m, :], pairs, [pk] + kk("xn", range(KC), [b]), [("ps", ps)])
                    if b < 4:
                        op("dve", (lambda b, hv, ps: lambda e: e.tensor_copy(vt[:, b, hv * 512:(hv + 1) * 512], psum[ps][:, :]))(b, hv, ps),
                           [("ps", ps)], [("vt", b, hv)], ov="2")
                    else:
                        op("dve", (lambda hv, ps: lambda e: e.tensor_copy(vs_s[:, hv * 512:(hv + 1) * 512], psum[ps][0:NS, :]))(hv, ps),
                           [("ps", ps)], [("vs_s", hv)])

            for b in pblks:
                bc0, m = blkcols(b)
                ps = newps()
                mm(psum[ps][:, :], [(lrT[0:17, bc0:bc0 + 128], wgu[0:17, :])], kk("lrT", [0], [b]) + ["c_wgu", "lrT_init"], [("ps", ps)])
                t1 = newtmp()
                lpb = lp[b % 2]
                op("act", (lambda t1, ps: lambda e: e.activation(tmps[t1][:, :], psum[ps][:, :], AF.Exp, scale=-1.0))(t1, ps), [("ps", ps)], [("tmp", t1)])
                op("act", (lambda t1, lpb: lambda e: e.activation(lpb[:, :], tmps[t1][:, :], AF.Ln, bias=1.0))(t1, lpb), [("tmp", t1)], [("lp", b % 2)])
                v_block(b)
                psb_ = newps()
                def fn(e, lpb=lpb, psb_=psb_):
                    ins = None
                    for hh in range(4):
                        ins = e.matmul(psum[psb_][:, hh * 128:(hh + 1) * 128], lpb[:, hh * 128:(hh + 1) * 128], Uneg[:], start=True, stop=True)
                    return ins
                p.add("pe", fn, [("lp", b % 2), "Uneg"], [("ps", psb_)])
                pv = psum[psb_][:, :].rearrange("p (g t) -> p g t", g=4)
                op("act", (lambda pv, bc0: lambda e: e.activation(ETv[:, :, bc0:bc0 + 128], pv, AF.Exp, bias=-0.5 * math.log(128.0)))(pv, bc0),
                   [("ps", psb_)], kk("ET", range(4), [b]), ov="1")
                op("act", (lambda pv, bc0: lambda e: e.activation(EIv[:, :, bc0:bc0 + 128], pv, AF.Exp, scale=-1.0))(pv, bc0),
                   [("ps", psb_)], kk("EI", range(4), [b]), ov="1")
                op("act", (lambda psb_, b: lambda e: e.activation(alast[:, b, :], psum[psb_][:, 127:512:128], AF.Exp))(psb_, b),
                   [("ps", psb_)], [("alast", b)])
                psr = newps()
                mm(psum[psr][:, :], [(Lneg[:], lpb[:, :])], [("lp", b % 2), "Lneg"], [("ps", psr)])
                op("act", (lambda psr, b: lambda e: e.activation(Krev[:, b, :], psum[psr][:, :], AF.Exp))(psr, b),
                   [("ps", psr)], [("Krev", b)], ov="1")
            if has_s:
                ps = newps()
                def fn(e, ps=ps):
                    ins = None
                    for hh in range(4):
                        ins = e.matmul(psum[ps][:, hh * 16:(hh + 1) * 16], wgu[0:17, hh * 128:(hh + 1) * 128], lrT[0:17, PT:TT], start=True, stop=True)
                    return ins
                p.add("pe", fn, kk("lrT", [0], [4]) + ["c_wgu", "lrT_init"], [("ps", ps)])
                op("act", (lambda ps: lambda e: e.activation(lpTs[:, :], psum[ps][:, 0:64], AF.Exp, scale=-1.0))(ps), [("ps", ps)], ["lpTs"])
                op("act", lambda e: e.activation(lpTs[:, :], lpTs[:, :], AF.Ln, bias=1.0), ["lpTs"], ["lpTs"])
                op("act", lambda e: e.activation(aTs[:, :], lpTs[:, :], AF.Exp, scale=-1.0 / 16.0), ["lpTs"], ["aTs"])

            if has_s:
                v_block(4)
            gbp = [load_piece(wview(win_d, KC, 3072 + i * 512, 512), KC, 512) for i in range(2)]
            for c in range(KC):
                pc, pk = gbp[c // 4]
                for seg in segs:
                    c0, n, blks = seg
                    ps = newps()
                    proj_fm(pc, pk, (c % 4) * 128, xn, "xn", KC, seg, ps)
                    op("act", (lambda c, c0, n, ps: lambda e: e.activation(sgb[:, c, c0:c0 + n], psum[ps][:, 0:n], AF.Silu))(c, c0, n, ps),
                       [("ps", ps)], kk("sgb", [c], blks), ov="A")

            for b in pblks:
                bc0, m = blkcols(b)
                ps = newps()
                def fn(e, b=b, ps=ps):
                    ins = None
                    for g in range(4):
                        e.matmul(psum[ps][:, g * 128:(g + 1) * 128], vn[:, b, g * 128:(g + 1) * 128], WsT[:, g, :], start=True, stop=False)
                        ins = e.matmul(psum[ps][:, g * 128:(g + 1) * 128], ones_b[0:1, :], brow[0:1, g, :], start=False, stop=True)
                    return ins
                p.add("pe", fn, [("vn", b), "WsT", "c_brow", "ones_b"], [("ps", ps)], ov="1")
                op("dve", (lambda ps, bc0: lambda e: e.tensor_tensor(
                    uT[:, :, bc0:bc0 + 128], psum[ps][:, :].rearrange("p (g t) -> p g t", g=4), uT[:, :, bc0:bc0 + 128], ALU.mult))(ps, bc0),
                   [("ps", ps)] + kk("uT", range(4), [b]), kk("uT", range(4), [b]), ov="1")
            if has_s:
                for g in range(4):
                    op("dve", (lambda g: lambda e: e.tensor_scalar(Dg[:, g, :], ident[0:16, 0:16], w00[0:16, g:g + 1], None, ALU.mult))(g),
                       ["ident", "c_w00"], [("Dg", g)])
                ps = newps()
                def fn(e, ps=ps):
                    ins = None
                    for g in range(4):
                        ins = e.matmul(psum[ps][:, g * 16:(g + 1) * 16], vn[0:NS, 4, g * 128:(g + 1) * 128], Dg[:, g, :], start=True, stop=True)
                    return ins
                p.add("pe", fn, [("vn", 4)] + [("Dg", g) for g in range(4)], [("ps", ps)], ov="1")
                for g in range(4):
                    op("dve", (lambda ps, g: lambda e: e.scalar_tensor_tensor(
                        uT[:, g, PT:TT], psum[ps][:, g * 16:(g + 1) * 16], bcol[:, g:g + 1], uT[:, g, PT:TT], ALU.add, ALU.mult))(ps, g),
                       [("ps", ps), "c_bcol"] + kk("uT", [g], [4]), kk("uT", [g], [4]), ov="1")

            ga = [load_piece(wview(win_d, KC, 4112 + i * 512, 512), KC, 512) for i in range(2)]
            wpa = load_piece(wview(wpa_d, 4, 0, 1024), 4, 1024)
            def fin_a(c, seg, sg, ti, ps2, si):
                c0, n, blks = seg
                op("dve", (lambda: lambda e: e.tensor_tensor(ma[:, c, c0:c0 + n], sg, psum[ps2][:, 0:n], ALU.mult))(),
                   [("tmp", ti), ("ps", ps2)], kk("ma", [c], blks), ov="A")
            gated_proj(xT, xb, segs, ga, [wpa], 4, uT, "uT", fin_a)

            act_prefetch_ln()
            pc, pk = load_piece(wview(win_d, KC, 1024, 512), KC, 512)
            for hh in range(4):
                for seg in segs:
                    c0, n, blks = seg
                    ps = newps()
                    proj_fm(pc, pk, hh * 128, xn, "xn", KC, seg, ps)
                    op("dve", (lambda hh, c0, n, ps: lambda e: e.tensor_tensor(qdT[:, hh, c0:c0 + n], psum[ps][:, 0:n], ETv[:, hh, c0:c0 + n], ALU.mult))(hh, c0, n, ps),
                       [("ps", ps)] + kk("ET", [hh], blks), kk("qdT", [hh], blks), ov="12")
            pc, pk = load_piece(wview(win_d, KC, 1536, 512), KC, 512)
            for hh in range(4):
                for seg in segs:
                    c0, n, blks = seg
                    ps = newps()
                    proj_fm(pc, pk, hh * 128, xn, "xn", KC, seg, ps)
                    op("dve", (lambda hh, c0, n, ps: lambda e: e.tensor_tensor(kdT[:, hh, c0:c0 + n], psum[ps][:, 0:n], EIv[:, hh, c0:c0 + n], ALU.mult))(hh, c0, n, ps),
                       [("ps", ps)] + kk("EI", [hh], blks), kk("kdT", [hh], blks), ov="12")
            for b in blks_all:
                bc0, m = blkcols(b)
                ps = newps()
                pairs = [(xn[:, kc, bc0:bc0 + m], pc[:, kc, :]) for kc in range(KC)]
                mm(psum[ps][0:m, :], pairs, [pk] + kk("xn", range(KC), [b]), [("ps", ps)])
                if b < 4:
                    op("dve", (lambda b, ps: lambda e: e.tensor_tensor(kst[:, b, :], psum[ps][:, :], Krev[:, b, :], ALU.mult))(b, ps),
                       [("ps", ps), ("Krev", b)], [("kst", b)], ov="12")
                else:
                    op("dve", (lambda ps: lambda e: e.tensor_copy(ks_s[:, :], psum[ps][0:NS, :]))(ps), [("ps", ps)], ["ks_s"])
            def epi_squares(pso, w, blk):
                si = blk % 2
                sq = sqb[si]
                for bk in range(2):
                    op("act", (lambda bk, sq, w: lambda e: e.activation(sq[:, bk * 4 * w:(bk + 1) * 4 * w], psum[pso[bk]][:, 0:4 * w], AF.Square))(bk, sq, w),
                       [("ps", pso[bk])], [("sq", si, bk)])

            def epi_rest(pso, w, cs0, blk):
                si = blk % 2
                sq = sqb[si]
                pss = newps()
                def fn(e, sq=sq, pss=pss, w=w):
                    ins = None
                    for hh in range(4):
                        for dvc in range(2):
                            r = (hh * 2 + dvc) * w
                            ins = e.matmul(psum[pss][:, hh * w:(hh + 1) * w], ones_b[:], sq[:, r:r + w], start=(dvc == 0), stop=(dvc == 1))
                    return ins
                p.add("pe", fn, [("sq", si, 0), ("sq", si, 1), "ones_b"], [("ps", pss)])
                t1 = newtmp()
                op("act", (lambda t1, pss, w: lambda e: e.activation(tmps[t1][:, 0:4 * w], psum[pss][:, 0:4 * w], AF.Ln, bias=EPS, scale=1.0 / 256.0))(t1, pss, w),
                   [("ps", pss)], [("tmp", t1)])
                op("act", (lambda t1, w: lambda e: e.activation(tmps[t1][:, 0:4 * w], tmps[t1][:, 0:4 * w], AF.Exp, scale=-0.5))(t1, w),
                   [("tmp", t1)], [("tmp", t1)])
                for bk in range(2):
                    t2 = newtmp()
                    rb = tmps[t1][:, bk * 2 * w:(bk + 1) * 2 * w].rearrange("p (h t) -> p h t", h=2).unsqueeze(2).to_broadcast([128, 2, 2, w])
                    op("dve", (lambda bk, t2, rb: lambda e: e.tensor_tensor(
                        tmps[t2][:, 0:4 * w].rearrange("p (h d t) -> p h d t", h=2, d=2),
                        psum[pso[bk]][:, 0:4 * w].rearrange("p (h d t) -> p h d t", h=2, d=2), rb, ALU.mult))(bk, t2, rb),
                       [("ps", pso[bk]), ("tmp", t1)], [("tmp", t2)])
                    op("dve", (lambda bk, t2: lambda e: e.tensor_tensor(
                        sgb[:, bk * 4:(bk + 1) * 4, cs0:cs0 + w], tmps[t2][:, 0:4 * w].rearrange("p (c t) -> p c t", c=4),
                        sgb[:, bk * 4:(bk + 1) * 4, cs0:cs0 + w], ALU.mult))(bk, t2),
                       [("tmp", t2)] + kk("sgb", range(bk * 4, bk * 4 + 4), [blk]), kk("sgb", range(bk * 4, bk * 4 + 4), [blk]), ov="A")

            def o_epilogue(pso, w, cs0, blk):
                epi_squares(pso, w, blk)
                epi_rest(pso, w, cs0, blk)

            def prompt_recurrence(bgstep):
                def scores_mask(b):
                    bc0 = b * 128
                    pss_ = newps()

                    def fn(e):
                        ins = None
                        for hh in range(4):
                            ins = e.matmul(psum[pss_][:, hh * 128:(hh + 1) * 128], kdT[:, hh, bc0:bc0 + 128], qdT[:, hh, bc0:bc0 + 128], start=True, stop=True)
                        return ins
                    p.add("pe", fn, kk("kdT", range(4), [b]) + kk("qdT", range(4), [b]), [("ps", pss_)], ov="2")
                    PTt = PTb[b % 2]
                    op("dve", lambda e: e.tensor_tensor(PTt[:, :], psum[pss_][:, :], maskT[:].rearrange("p g t -> p (g t)"), ALU.mult),
                       [("ps", pss_), "maskT"], [("PT", b % 2)])

                def s_chain(b):
                    gb = t * 4 + b
                    psu_ = [newps(), newps()]

                    def fn(e):
                        ins = None
                        for hh in range(4):
                            ins = e.matmul(psum[psu_[hh // 2]][:, (hh % 2) * 256:(hh % 2) * 256 + 256], kst[:, b, hh * 128:(hh + 1) * 128], vt[:, b, hh * 256:(hh + 1) * 256], start=True, stop=True)
                        return ins
                    p.add("pe", fn, [("kst", b), ("vt", b, 0), ("vt", b, 1)], [("ps", psu_[0]), ("ps", psu_[1])], ov="2")
                    for hh in range(4):
                        src = psum[psu_[hh // 2]][:, (hh % 2) * 256:(hh % 2) * 256 + 256]
                        if gb == 0:
                            op("dve", (lambda hh, src: lambda e: e.tensor_copy(Sf[:, hh, :], src))(hh, src), [("ps", psu_[hh // 2])], [("Sf", hh)])
                        else:
                            op("dve", (lambda hh, src: lambda e: e.scalar_tensor_tensor(Sf[:, hh, :], Sf[:, hh, :], alast[:, b, hh:hh + 1], src, ALU.mult, ALU.add))(hh, src),
                               [("ps", psu_[hh // 2]), ("alast", b), ("Sf", hh)], [("Sf", hh)])
                def s_copy(b):
                    gb = t * 4 + b
                    if gb < 15:
                        Snew = Sbs[gb % 2]
                        op("act", lambda e: e.activation(Snew[:].rearrange("p h v -> p (h v)"), Sf[:].rearrange("p h v -> p (h v)"), AF.Copy),
                           [("Sf", hh) for hh in range(4)], [("Sb", gb % 2)])
                    else:
                        p.add("sp", lambda e: e.dma_start(out=sp_d.rearrange("h k v -> k h v"), in_=Sf[:]), reads=[("Sf", hh) for hh in range(4)], dma_sem="spo")

                s_chain(pblks[0])
                s_copy(pblks[0])
                scores_mask(pblks[0])
                for bi, b in enumerate(pblks):
                    gb = t * 4 + b
                    bc0 = b * 128
                    PTt = PTb[b % 2]
                    Sprev = Sbs[(gb - 1) % 2]
                    pso = [newps(), newps()]
                    ps_reserved.update(pso)

                    def fn(e, pso=pso, PTt=PTt, b=b, bc0=bc0, gb=gb, Sprev=Sprev):
                        ins = None
                        for hh in range(4):
                            for dvc in range(2):
                                r = ((hh % 2) * 2 + dvc) * 128
                                o = psum[pso[hh // 2]][:, r:r + 128]
                                ins = e.matmul(o, vt[:, b, hh * 256 + dvc * 128:hh * 256 + dvc * 128 + 128], PTt[:, hh * 128:(hh + 1) * 128], start=True, stop=(gb == 0))
                                if gb > 0:
                                    ins = e.matmul(o, Sprev[:, hh, dvc * 128:(dvc + 1) * 128], qdT[:, hh, bc0:bc0 + 128], start=False, stop=True)
                        return ins
                    p.add("pe", fn, [("vt", b, 0), ("vt", b, 1), ("PT", b % 2), ("Sb", (gb - 1) % 2)] + kk("qdT", range(4), [b]), [("ps", pso[0]), ("ps", pso[1])], ov="2")
                    epi_squares(pso, 128, b)
                    if bi + 1 < len(pblks):
                        s_chain(pblks[bi + 1])
                        scores_mask(pblks[bi + 1])
                    bgstep()
                    epi_rest(pso, 128, bc0, b)
                    ps_reserved.difference_update(pso)
                    if bi + 1 < len(pblks):
                        s_copy(pblks[bi + 1])
                    if bi == 1:
                        scale_wpb()
                    bgstep()

            def sample_steps():
                psos = [newps(), newps()]
                ps_reserved.update(psos)

                alias_deps = set()
                for key in [("uT", c, b_) for c in range(4) for b_ in range(5)] + [("vn", b_) for b_ in range(5)]:
                    if key in p.lastw:
                        alias_deps.add(p.lastw[key])
                    alias_deps.update(p.readers.get(key, ()))
                alias_deps = p._prune(alias_deps)

                def load_s0(m_):
                    sj = m_ % 4
                    p.add("sp", lambda e: e.dma_start(out=s0b[sj], in_=st_d[m_].rearrange("h k v -> k h v")),
                          writes=[("s0", sj)], dma_sem=f"s0l{sj}", ov=s0reg[sj],
                          extra_deps=(alias_deps if m_ in (2, 3) else ()))

                def make_kmask(m_):
                    kmm = kmask[m_ % 2]
                    op("dve", lambda e: e.tensor_scalar(kmm[:, :], ks_s[:, :], ident[0:16, m_:m_ + 1], None, ALU.mult),
                       ["ks_s", "ident"], [("km", m_ % 2)])

                def stage_a(n_):
                    si = n_ % 2
                    sq_ = n_ % 4
                    s0 = s0b[sq_]
                    if n_ == 0:
                        load_s0(0)
                        load_s0(1)
                        load_s0(2)
                    if n_ + 3 < NS:
                        load_s0(n_ + 3)
                    km = kmask[si]
                    if n_ == 0:
                        make_kmask(0)
                    psu_ = [newps(), newps()]

                    def fn(e):
                        ins = None
                        for hh in range(4):
                            ins = e.matmul(psum[psu_[hh // 2]][:, (hh % 2) * 256:(hh % 2) * 256 + 256], km[:, hh * 128:(hh + 1) * 128], vs_s[:, hh * 256:(hh + 1) * 256], start=True, stop=True)
                        return ins
                    p.add("pe", fn, [("km", si), ("vs_s", 0), ("vs_s", 1)], [("ps", psu_[0]), ("ps", psu_[1])])
                    for hh in range(4):
                        src = psum[psu_[hh // 2]][:, (hh % 2) * 256:(hh % 2) * 256 + 256]
                        op("dve", (lambda hh, src: lambda e: e.scalar_tensor_tensor(s0[:, hh, :], s0[:, hh, :], aTs[:, hh * 16 + n_:hh * 16 + n_ + 1], src, ALU.mult, ALU.add))(hh, src),
                           [("ps", psu_[hh // 2]), "aTs", ("s0", sq_)], [("s0", sq_)], ov=s0reg[sq_])
                    if n_ + 1 < NS:
                        make_kmask(n_ + 1)
                    p.add("sp", lambda e: e.dma_start(out=ss_d[n_].rearrange("h k v -> k h v"), in_=s0),
                          reads=[("s0", sq_)], dma_sem=f"s0s{sq_}", ov=s0reg[sq_])
                    sn = snb[si]
                    op("act", lambda e: e.activation(sn[:].rearrange("p h v -> p (h v)"), s0.rearrange("p h v -> p (h v)"), AF.Copy),
                       [("s0", sq_)], [("snb", si)], ov=s0reg[sq_])

                def stage_b(n_):
                    si = n_ % 2
                    sn = snb[si]

                    def fn(e):
                        ins = None
                        for hh in range(4):
                            for dvc in range(2):
                                r = ((hh % 2) * 2 + dvc) * NS + n_
                                ins = e.matmul(psum[psos[hh // 2]][:, r:r + 1], sn[:, hh, dvc * 128:(dvc + 1) * 128], qdT[:, hh, PT + n_:PT + n_ + 1], start=True, stop=True)
                        return ins
                    p.add("pe", fn, [("snb", si)] + kk("qdT", range(4), [4]), [("ps", psos[0]), ("ps", psos[1])], ov="2")

                for n_ in range(NS + 1):
                    if n_ < NS:
                        stage_a(n_)
                    if n_ >= 1:
                        stage_b(n_ - 1)
                    yield
                o_epilogue(psos, NS, PT, 4)
                ps_reserved.difference_update(psos)
                yield

            bg = sample_steps() if has_s else None

            def bgstep():
                if bg is not None:
                    next(bg, None)

            gbb = [load_piece(wview(win_d, KC, 5136 + i * 512, 512), KC, 512) for i in range(2)]
            wpb = [load_piece(wview(wpb_d, KC, i * 512, 512), KC, 512) for i in range(2)]

            def scale_wpb():
                for (wv, wk) in wpb:
                    for dvc in range(2):
                        op("dve", (lambda wv, dvc: lambda e: e.tensor_scalar(wv[:, dvc:KC:2, :], wv[:, dvc:KC:2, :], cols[:, C_GGLA + dvc:C_GGLA + dvc + 1], None, ALU.mult))(wv, dvc),
                           [wk, "c_cols"], [wk])
            if not has_s:
                p.fence("1")
                on_m11(range(2))
            prompt_recurrence(bgstep)

            def drain_bg():
                if bg is not None:
                    for _ in bg:
                        pass

            def m11_mid():
                drain_bg()
                if has_s:
                    p.fence("1")
                    p.fence("2")
                else:
                    p.fence("2")
                    on_m11(range(2, 6))
            if not has_s:
                m11_mid()
            def fin_b(c, seg, sg, ti, ps2, si):
                c0, n, blks = seg
                op("dve", (lambda: lambda e: e.tensor_tensor(sg, sg, psum[ps2][:, 0:n], ALU.mult))(),
                   [("tmp", ti), ("ps", ps2)], [("tmp", ti)])
                op("dve", (lambda: lambda e: e.tensor_tensor(ma[:, c, c0:c0 + n], sg, ma[:, c, c0:c0 + n], ALU.add))(),
                   [("tmp", ti)] + kk("ma", [c], blks), kk("ma", [c], blks), ov="A")
            if has_s:
                gated_proj(xT, xb, segs, gbb, wpb, KC, sgb, "sgb", fin_b, segmajor=True, hook=bgstep, mid=m11_mid)
            else:
                gated_proj(xT, xb, segs, gbb, wpb, KC, sgb, "sgb", fin_b)

            wop = [load_piece(wview(wout_d, KC, i * 512, 512), KC, 512) for i in range(2)]
            on_m12()
            act_prefetch_ln()
            stats_begin(segs, delay=2)
            for c in range(KC):
                pc, pk = wop[c // 4]
                for si, seg in enumerate(segs):
                    c0, n, blks = seg
                    ps = newps()
                    proj_fm(pc, pk, (c % 4) * 128, ma, "ma", KC, seg, ps, ov="A")
                    stats_pump()
                    op("dve", (lambda c, c0, n, ps: lambda e: e.tensor_tensor(xT[:, c, c0:c0 + n], psum[ps][:, 0:n], xT[:, c, c0:c0 + n], ALU.add))(c, c0, n, ps),
                       [("ps", ps)] + kk(("xT", xb), [c], blks), kk(("xT", xb), [c], blks))
                    stats_done(xT, xb, c, si, seg)
            p.fence("A")

        finals = []
        pre = {}
        xv = xT_d.rearrange("(k p) n -> p k n", p=128)
        pv_ = pT_d.rearrange("(k p) n -> p k n", p=128)

        def load_x(t):
            xb = t % 2
            xT = xTb[xb]
            p.add("sp", lambda e: e.dma_start(out=xT[:, :, 0:PT], in_=xv[:, :, t * PT:(t + 1) * PT]),
                  writes=kk(("xT", xb), range(KC), [0, 1, 2, 3]), dma_sem=f"xl{xb}")
            if t == 0:
                p.add("sp", lambda e: e.dma_start(out=xT[:, :, PT:TT], in_=xv[:, :, 2048:NTOK]),
                      writes=kk(("xT", xb), range(KC), [4]), dma_sem="xls")

        def load_p(t):
            xb = t % 2
            pTb = pTbs[xb]
            pkey = ("pTb", xb)
            p.add("pool", lambda e: e.dma_start(out=pTb[:, :, 0:PT], in_=pv_[:, :, t * PT:(t + 1) * PT]),
                  writes=kk(pkey, range(2), [0, 1, 2, 3]), dma_sem=f"pl{xb}")
            if t == 0:
                p.add("pool", lambda e: e.dma_start(out=pTb[:, :, PT:TT], in_=pv_[:, :, 2048:NTOK]),
                      writes=kk(pkey, range(2), [4]), dma_sem="pls")

        for t in range(NT):
            ncol, segs, blks_all = tile_info(t)
            xb = t % 2
            xT = xTb[xb]
            pTb = pTbs[xb]
            pkey = ("pTb", xb)
            if t == 0:
                load_x(1)
                pre["f1"] = pre0
            stats_begin(segs)
            for si, seg in enumerate(segs):
                for c in range(KC):
                    stats_feed(xT, xb, c, si, seg)

            ffn(xT, xb, segs, w1i_d, C_GFFN1, pre["f1"], wo_late=w1o_d)
            if t == 0:
                late_setup()
                load_p(0)
            if t + 1 < NT:
                load_p(t + 1)
            mixer(t, xT, xb, ncol, segs, blks_all,
                  on_m11=lambda which: load_wo(w2o_d, which),
                  on_m12=lambda: pre.__setitem__("f2", ffn_pieces(w2i_d, 0)))
            ffn(xT, xb, segs, w2i_d, C_GFFN2, pre["f2"], wo_late=(w2o_d if t == 0 else None))

            rmsnorm_to_xn(xT, xb, segs, C_GPLE)
            wpg = [load_piece(wview(wpg_d, KC, i * 512, 512), KC, 512) for i in range(2)]
            wpl = load_piece(wview(wple_d, 2, 0, 1024), 2, 1024)
            if t + 1 < NT:
                pre["f1"] = ffn_pieces(w1i_d, 0)
            def fin_p(c, seg, sg, ti, ps2, si, xT=xT, xb=xb):
                c0, n, blks = seg
                op("dve", (lambda: lambda e: e.tensor_tensor(sg, sg, psum[ps2][:, 0:n], ALU.mult))(),
                   [("tmp", ti), ("ps", ps2)], [("tmp", ti)])
                op("dve", (lambda: lambda e: e.tensor_tensor(xT[:, c, c0:c0 + n], sg, xT[:, c, c0:c0 + n], ALU.add))(),
                   [("tmp", ti)] + kk(("xT", xb), [c], blks), kk(("xT", xb), [c], blks))
                stats_done(xT, xb, c, si, seg)
            stats_begin(segs, delay=2)
            gated_proj(xT, xb, segs, wpg, [wpl], 2, pTb, pkey, fin_p, last_prefetch=True)

            stats_finish(segs)
            for (c0, n, blks) in segs:
                for kc in range(KC):
                    op("dve", (lambda kc, c0, n, xT=xT: lambda e: e.scalar_tensor_tensor(
                        xT[:, kc, c0:c0 + n], xT[:, kc, c0:c0 + n], cols[:, C_GFIN + kc:C_GFIN + kc + 1], rinv[:, c0:c0 + n],
                        ALU.mult, ALU.mult))(kc, c0, n),
                       kk(("xT", xb), [kc], blks) + kk("rinv", [0], blks) + ["c_cols"], kk(("xT", xb), [kc], blks))
            yv = yT_d.rearrange("(k p) n -> p k n", p=128)
            finals.append(p.add("sp", (lambda t, xT: lambda e: e.dma_start(out=yv[:, :, t * PT:(t + 1) * PT], in_=xT[:, :, 0:PT]))(t, xT),
                                reads=kk(("xT", xb), range(KC), [0, 1, 2, 3]), dma_sem=f"ys{xb}"))
            if t == 0:
                finals.append(p.add("sp", (lambda xT: lambda e: e.dma_start(out=yv[:, :, 2048:NTOK], in_=xT[:, :, PT:TT]))(xT),
                                    reads=kk(("xT", xb), range(KC), [4]), dma_sem="yss"))
            if t + 2 < NT:
                load_x(t + 2)

        outs = [i for i, o in enumerate(p.ops) if o["dma_sem"] is not None and
                (o["dma_sem"].startswith("ys") or o["dma_sem"] in ("cv", "spo") or o["dma_sem"].startswith("s0s"))]
        p.add("sp", lambda e: e.nop(), extra_deps=outs)
        p.emit(st)
    return nc


_CACHE = {}


def _prep_inputs(inp):
    f = np.float32
    xp = np.asarray(inp["x_prompt"], f)
    xs = np.asarray(inp["x_sample"], f)
    pp = np.asarray(inp["p_prompt"], f)[0]
    ps_ = np.asarray(inp["p_sample"], f)[0]
    stg = np.asarray(inp["state_gla"], f)[0]

    def colpack(g):
        return np.ascontiguousarray(np.asarray(g, f).reshape(-1, 128).T)

    cols = np.concatenate([colpack(inp["g_ffn1"][0]), colpack(inp["g_mix"][0]), colpack(inp["g_ffn2"][0]),
                           colpack(inp["g_ple"][0]), colpack(inp["g_final"]), colpack(inp["g_gla_out"][0])], axis=1)
    shared = {
        "cols": np.ascontiguousarray(cols, f),
        "w_ffn1_in": np.ascontiguousarray(inp["w_ffn1_in"][0], f),
        "w_ffn1_out": np.ascontiguousarray(inp["w_ffn1_out"][0], f),
        "w_in": np.ascontiguousarray(inp["w_in"][0], f),
        "ln_v_g": np.ascontiguousarray(inp["ln_v_g"][0], f),
        "ln_v_b": np.ascontiguousarray(inp["ln_v_b"][0], f),
        "wsT": np.ascontiguousarray(np.transpose(np.asarray(inp["w_spatial"][0], f), (0, 2, 1))),
        "w_spatial": np.ascontiguousarray(inp["w_spatial"][0], f),
        "b_spatial": np.ascontiguousarray(inp["b_spatial"][0], f),
        "w_gate_up": np.ascontiguousarray(inp["w_gate_up"][0], f),
        "b_gate": np.ascontiguousarray(inp["b_gate"][0], f),
        "w_proj_a": np.ascontiguousarray(inp["w_proj_a"][0], f),
        "w_proj_b": np.ascontiguousarray(inp["w_proj_b"][0], f),
        "w_out": np.ascontiguousarray(inp["w_out"][0], f),
        "w_ffn2_in": np.ascontiguousarray(inp["w_ffn2_in"][0], f),
        "w_ffn2_out": np.ascontiguousarray(inp["w_ffn2_out"][0], f),
        "w_ple_gate": np.ascontiguousarray(inp["w_ple_gate"][0], f),
        "w_ple": np.ascontiguousarray(inp["w_ple"][0], f),
    }
    maps = []
    for i in range(NCORES):
        sl = slice(NS * i, NS * (i + 1))
        xT = np.concatenate([xp[i].T, xs[sl, 0, :].T], axis=1)
        pT = np.concatenate([pp[i].T, ps_[sl, 0, :].T], axis=1)
        m = dict(shared)
        m["xT"] = np.ascontiguousarray(xT, f)
        m["pT"] = np.ascontiguousarray(pT, f)
        m["st"] = np.ascontiguousarray(stg[sl], f)
        maps.append(m)
    return maps


def kernel(**inputs):
    if "nc" not in _CACHE:
        _CACHE["nc"] = build_program()
    nc = _CACHE["nc"]
    maps = _prep_inputs(inputs)
    res = run_bass_kernel_spmd(nc, maps, core_ids=list(range(NCORES)))
    r = res.results
    y_prompt = np.stack([r[i]["yT"][:, :2048].T for i in range(NCORES)]).astype(np.float32)
    y_sample = np.concatenate([r[i]["yT"][:, 2048:].T for i in range(NCORES)])[:, None, :].astype(np.float32)
    sp = np.stack([r[i]["sp_out"] for i in range(NCORES)])[None].astype(np.float32)
    ss = np.concatenate([r[i]["ss_out"] for i in range(NCORES)])[None].astype(np.float32)
    cv = np.concatenate([r[i]["cv_out"] for i in range(NCORES)])[None, :, None, :].astype(np.float32)
    return (np.ascontiguousarray(y_prompt), np.ascontiguousarray(y_sample), np.ascontiguousarray(sp),
            np.ascontiguousarray(ss), np.ascontiguousarray(cv))
```
